# Optimizing a Trainium2 kernel written in Bass

```python
import math
import jax, jax.numpy as jnp
from jax import lax
import numpy as np

D_MODEL = 1024
BATCH = 8
SEQ = 2048
DEPTH = 4

N_MIXERS = 2
N_GLA = (DEPTH + 1) // 2
N_MLA = DEPTH // 2
EPS = 1e-6

D_FF = 2816

GLA_HEADS = 4
GLA_DK_TOT = D_MODEL // 2
GLA_DV_TOT = D_MODEL
GLA_DK = GLA_DK_TOT // GLA_HEADS
GLA_DV = GLA_DV_TOT // GLA_HEADS
GLA_GATE_RANK = 16
GLA_TAU = 16.0
GLA_CHUNK = 64
GLA_IN = 2 * GLA_DK_TOT + 2 * GLA_DV_TOT + 2 * GLA_GATE_RANK

MLA_HEADS = 8
MLA_NOPE = 128
MLA_ROPE = 64
MLA_V = 128
MLA_Q_RANK = 768
MLA_KV_RANK = 256
MLA_QK = MLA_NOPE + MLA_ROPE
MLA_IN = MLA_Q_RANK + MLA_KV_RANK + MLA_ROPE
ROPE_THETA = 10000.0
Q_BLOCK = 128
MAX_POS_OFFSET = 4096

kernel_name = "hybrid_gla_mla_macaron_encoder"


def rms_norm(x, g):
    xf = x.astype(jnp.float32)
    y = xf * lax.rsqrt(jnp.mean(xf * xf, axis=-1, keepdims=True) + EPS)
    return (y * g.astype(jnp.float32)).astype(x.dtype)


def swiglu(h, w_gu, w_down):
    gate, up = jnp.split(h @ w_gu, [D_FF], axis=-1)
    return (jax.nn.silu(gate) * up) @ w_down


def gla_direction(q, k, v, log_a, include_diag):
    bsz, nh, seq, dk = q.shape
    dv = v.shape[-1]
    n = seq // GLA_CHUNK

    def to_chunks(t):
        return jnp.moveaxis(t.reshape(bsz, nh, n, GLA_CHUNK, t.shape[-1]), 2, 0)

    qc, kc, vc = to_chunks(q), to_chunks(k), to_chunks(v)
    bc = jnp.cumsum(to_chunks(log_a), axis=-2)
    mask = jnp.tril(jnp.ones((GLA_CHUNK, GLA_CHUNK), dtype=bool), k=0 if include_diag else -1)

    def step(state, inp):
        q_, k_, v_, b_ = inp
        diff = b_[:, :, :, None, :] - b_[:, :, None, :, :]
        decay = jnp.exp(jnp.where(mask[:, :, None], diff, -jnp.inf))
        scores = jnp.einsum('bhtd,bhsd,bhtsd->bhts', q_, k_, decay)
        out = (jnp.einsum('bhts,bhsv->bhtv', scores, v_)
               + jnp.einsum('bhtd,bhdv->bhtv', q_ * jnp.exp(b_), state))
        b_end = b_[:, :, -1:, :]
        state = (state * jnp.exp(b_end)[:, :, 0, :, None]
                 + jnp.einsum('bhsd,bhsv->bhdv', k_ * jnp.exp(b_end - b_), v_))
        return state, out

    state0 = jnp.zeros((bsz, nh, dk, dv), jnp.float32)
    _, out = lax.scan(step, state0, (qc, kc, vc, bc))
    return jnp.moveaxis(out, 0, 2).reshape(bsz, nh, seq, dv)


def gla_mixer(h, w_in, w_gate2, b_gate, head_norm, w_out):
    bsz, seq, _ = h.shape
    splits = [GLA_DK_TOT, 2 * GLA_DK_TOT, 2 * GLA_DK_TOT + GLA_DV_TOT,
              2 * GLA_DK_TOT + 2 * GLA_DV_TOT, 2 * GLA_DK_TOT + 2 * GLA_DV_TOT + GLA_GATE_RANK]
    q, k, v, r, g_fw, g_bw = jnp.split(h @ w_in, splits, axis=-1)

    def heads(t, d):
        return t.reshape(bsz, seq, GLA_HEADS, d).transpose(0, 2, 1, 3).astype(jnp.float32)

    q = heads(q, GLA_DK) * (GLA_DK ** -0.5)
    k = heads(k, GLA_DK)
    v = heads(v, GLA_DV)
    log_fw = heads(jax.nn.log_sigmoid(g_fw @ w_gate2[0] + b_gate[0]), GLA_DK) / GLA_TAU
    log_bw = heads(jax.nn.log_sigmoid(g_bw @ w_gate2[1] + b_gate[1]), GLA_DK) / GLA_TAU

    flip = lambda t: t[:, :, ::-1, :]
    o_fw = gla_direction(q, k, v, log_fw, True)
    o_bw = flip(gla_direction(flip(q), flip(k), flip(v), flip(log_bw), False))
    o = (o_fw + o_bw).transpose(0, 2, 1, 3)
    o = rms_norm(o, head_norm).astype(h.dtype)
    o = o * jax.nn.silu(r).reshape(bsz, seq, GLA_HEADS, GLA_DV)
    return o.reshape(bsz, seq, GLA_DV_TOT) @ w_out


def apply_rope(x, cos, sin):
    c = cos[:, :, None, :].astype(x.dtype)
    s = sin[:, :, None, :].astype(x.dtype)
    x1, x2 = jnp.split(x, 2, axis=-1)
    return jnp.concatenate([x1 * c - x2 * s, x2 * c + x1 * s], axis=-1)


def mla_mixer(h, cos, sin, w_in, q_norm, kv_norm, w_uq, w_ukv, w_out):
    bsz, seq, _ = h.shape
    c_q, c_kv, k_pe = jnp.split(h @ w_in, [MLA_Q_RANK, MLA_Q_RANK + MLA_KV_RANK], axis=-1)
    q = (rms_norm(c_q, q_norm) @ w_uq).reshape(bsz, seq, MLA_HEADS, MLA_QK)
    q_nope, q_pe = jnp.split(q, [MLA_NOPE], axis=-1)
    kv = (rms_norm(c_kv, kv_norm) @ w_ukv).reshape(bsz, seq, MLA_HEADS, MLA_NOPE + MLA_V)
    k_nope, v = jnp.split(kv, [MLA_NOPE], axis=-1)
    q_pe = apply_rope(q_pe, cos, sin)
    k_pe = apply_rope(k_pe[:, :, None, :], cos, sin)
    q_full = jnp.concatenate([q_nope, q_pe], axis=-1).transpose(0, 2, 1, 3)
    k_full = jnp.concatenate(
        [k_nope, jnp.broadcast_to(k_pe, (bsz, seq, MLA_HEADS, MLA_ROPE))], axis=-1
    ).transpose(0, 2, 1, 3)
    v = v.transpose(0, 2, 1, 3)
    scale = MLA_QK ** -0.5
    n_blk = seq // Q_BLOCK
    q_blocks = jnp.moveaxis(q_full.reshape(bsz, MLA_HEADS, n_blk, Q_BLOCK, MLA_QK), 2, 0)

    def attend(qb):
        s = jnp.einsum('bhqd,bhkd->bhqk', qb, k_full).astype(jnp.float32) * scale
        p = jax.nn.softmax(s, axis=-1).astype(v.dtype)
        return jnp.einsum('bhqk,bhkv->bhqv', p, v)

    out = lax.map(attend, q_blocks)
    out = jnp.moveaxis(out, 0, 2).reshape(bsz, MLA_HEADS, seq, MLA_V)
    out = out.transpose(0, 2, 1, 3).reshape(bsz, seq, MLA_HEADS * MLA_V)
    return out @ w_out


def setup_inputs(seed: int = 0) -> dict:
    key = jax.random.key(seed)
    ks = jax.random.split(key, 24)

    def dense(k, shape, fan_in):
        return jax.random.normal(k, shape, jnp.float32) * (fan_in ** -0.5)

    def gain(k, shape):
        return 1.0 + 0.02 * jax.random.normal(k, shape, jnp.float32)

    x = jax.random.normal(ks[0], (BATCH, SEQ, D_MODEL), jnp.float32)
    positions = (jnp.arange(SEQ, dtype=jnp.int32)[None, :]
                 + jax.random.randint(ks[1], (BATCH, 1), 0, MAX_POS_OFFSET, dtype=jnp.int32))
    return {
        "x": x,
        "positions": positions,
        "ffn_norm": gain(ks[2], (DEPTH, 2, D_MODEL)),
        "ffn_w_gu": dense(ks[3], (DEPTH, 2, D_MODEL, 2 * D_FF), D_MODEL),
        "ffn_w_down": dense(ks[4], (DEPTH, 2, D_FF, D_MODEL), D_FF),
        "mix_norm": gain(ks[5], (DEPTH, D_MODEL)),
        "gla_w_in": dense(ks[6], (N_GLA, D_MODEL, GLA_IN), D_MODEL),
        "gla_w_gate2": dense(ks[7], (N_GLA, 2, GLA_GATE_RANK, GLA_DK_TOT), GLA_GATE_RANK),
        "gla_b_gate": 0.1 * jax.random.normal(ks[8], (N_GLA, 2, GLA_DK_TOT), jnp.float32),
        "gla_head_norm": gain(ks[9], (N_GLA, GLA_DV)),
        "gla_w_out": dense(ks[10], (N_GLA, GLA_DV_TOT, D_MODEL), GLA_DV_TOT),
        "mla_w_in": dense(ks[11], (N_MLA, D_MODEL, MLA_IN), D_MODEL),
        "mla_q_norm": gain(ks[12], (N_MLA, MLA_Q_RANK)),
        "mla_kv_norm": gain(ks[13], (N_MLA, MLA_KV_RANK)),
        "mla_w_uq": dense(ks[14], (N_MLA, MLA_Q_RANK, MLA_HEADS * MLA_QK), MLA_Q_RANK),
        "mla_w_ukv": dense(ks[15], (N_MLA, MLA_KV_RANK, MLA_HEADS * (MLA_NOPE + MLA_V)), MLA_KV_RANK),
        "mla_w_out": dense(ks[16], (N_MLA, MLA_HEADS * MLA_V, D_MODEL), MLA_HEADS * MLA_V),
        "final_norm": gain(ks[17], (D_MODEL,)),
    }


def reference(x, positions, ffn_norm, ffn_w_gu, ffn_w_down, mix_norm,
              gla_w_in, gla_w_gate2, gla_b_gate, gla_head_norm, gla_w_out,
              mla_w_in, mla_q_norm, mla_kv_norm, mla_w_uq, mla_w_ukv, mla_w_out,
              final_norm):
    inv_freq = 1.0 / (ROPE_THETA ** (jnp.arange(0, MLA_ROPE, 2, dtype=jnp.float32) / MLA_ROPE))
    ang = positions.astype(jnp.float32)[..., None] * inv_freq
    cos, sin = jnp.cos(ang), jnp.sin(ang)

    for i in range(DEPTH):
        x = x + 0.5 * swiglu(rms_norm(x, ffn_norm[i, 0]), ffn_w_gu[i, 0], ffn_w_down[i, 0])
        h = rms_norm(x, mix_norm[i])
        j = i // N_MIXERS
        if i % N_MIXERS == 0:
            x = x + gla_mixer(h, gla_w_in[j], gla_w_gate2[j], gla_b_gate[j],
                              gla_head_norm[j], gla_w_out[j])
        else:
            x = x + mla_mixer(h, cos, sin, mla_w_in[j], mla_q_norm[j], mla_kv_norm[j],
                              mla_w_uq[j], mla_w_ukv[j], mla_w_out[j])
        x = x + 0.5 * swiglu(rms_norm(x, ffn_norm[i, 1]), ffn_w_gu[i, 1], ffn_w_down[i, 1])
    return rms_norm(x, final_norm)
```

```python
from collections import defaultdict
from contextlib import ExitStack
import math
import os
import numpy as np
import concourse.bass as bass
import concourse.mybir as mybir
from concourse.bass_utils import run_bass_kernel_spmd

F32 = mybir.dt.float32
BF16 = mybir.dt.bfloat16
I32 = mybir.dt.int32
ALU = mybir.AluOpType
AF = mybir.ActivationFunctionType

ENGS = ("pe", "act", "dve", "pool", "sp")

D = 1024
S = 2048
DEPTH = 4
DFF = 2816
NT = 4
TT = 512
KC = 8
EPS = 1e-6


class Op:
    __slots__ = ("eng", "fn", "waits", "signal", "idx", "dma_sem", "dma_val", "ordinal")

    def __init__(self, eng, fn, idx):
        self.eng = eng
        self.fn = fn
        self.idx = idx
        self.waits = {}
        self.signal = False
        self.dma_sem = None
        self.dma_val = 0
        self.ordinal = 0


class Prog:
    def __init__(self, nc, stack):
        self.nc = nc
        self.stack = stack
        self.ops = {e: [] for e in ENGS}
        self.lastw = {}
        self.readers = {}
        self.bufkeys = defaultdict(set)
        self.seen = {e: {} for e in ENGS}
        self.dma_cum = {}
        self.raw_window = 2
        self.full_same_engine_sync = os.environ.get("K_FULLSYNC", "0") == "1"

    def sb(self, name, shape, dt):
        return self.stack.enter_context(self.nc.sbuf_tensor("sb_" + name, list(shape), dt))

    def ps(self, name, shape, dt=F32):
        return self.stack.enter_context(self.nc.psum_tensor("pp_" + name, list(shape), dt))

    def _conf(self, key):
        n = len(key)
        for k2 in self.bufkeys.get(key[0], ()):
            m = len(k2)
            if m <= n:
                if key[:m] == k2:
                    yield k2
            elif k2[:n] == key:
                yield k2

    def _need(self, op, d, kind):
        if d is op:
            return
        if d.dma_sem is not None:
            src = ("dma", d.dma_sem)
            val = d.dma_val
        else:
            src = d.eng
            val = d.idx
            if d.eng == op.eng:
                if op.eng == "pe" or op.eng == "sp":
                    return
                if not self.full_same_engine_sync:
                    if kind != "raw":
                        return
                    if op.idx - d.idx > self.raw_window:
                        return
        if self.seen[op.eng].get(src, -1) >= val:
            return
        self.seen[op.eng][src] = val
        op.waits[src] = val
        d.signal = True

    def add(self, eng, fn, r=(), w=(), dma_sem=None):
        op = Op(eng, fn, len(self.ops[eng]))
        if dma_sem is not None:
            op.dma_sem = dma_sem
            self.dma_cum[dma_sem] = self.dma_cum.get(dma_sem, 0) + 16
            op.dma_val = self.dma_cum[dma_sem]
        r = [tuple(k) if isinstance(k, (tuple, list)) else (k,) for k in r]
        w = [tuple(k) if isinstance(k, (tuple, list)) else (k,) for k in w]
        for k in r:
            for k2 in self._conf(k):
                d = self.lastw.get(k2)
                if d is not None:
                    self._need(op, d, "raw")
                if k[0] == "ps":
                    for d in self.readers.get(k2, {}).values():
                        if d.eng != eng:
                            self._need(op, d, "rar")
        for k in w:
            for k2 in list(self._conf(k)):
                d = self.lastw.get(k2)
                if d is not None:
                    self._need(op, d, "waw")
                for d in self.readers.get(k2, {}).values():
                    self._need(op, d, "war")
        rid = eng if dma_sem is None else ("dma", dma_sem)
        for k in r:
            self.bufkeys[k[0]].add(k)
            self.readers.setdefault(k, {})[rid] = op
        for k in w:
            n = len(k)
            for k2 in list(self._conf(k)):
                if len(k2) >= n:
                    self.lastw[k2] = op
                    self.readers[k2] = {}
            self.bufkeys[k[0]].add(k)
            self.lastw[k] = op
            self.readers[k] = {}
        self.ops[eng].append(op)
        return op

    def pe(self, fn, r=(), w=()):
        return self.add("pe", fn, r, w)

    def act(self, fn, r=(), w=()):
        return self.add("act", fn, r, w)

    def dve(self, fn, r=(), w=()):
        return self.add("dve", fn, r, w)

    def pool(self, fn, r=(), w=()):
        return self.add("pool", fn, r, w)

    def dma(self, q, out, in_, r=(), w=(), sem=None, **kw):
        return self.add(q, lambda e: e.dma_start(out=out, in_=in_, **kw), r, w, dma_sem=sem)

    def fence(self, region):
        return self.add("sp", lambda e: e.nop(), r=(), w=[(region,)])

    def final_wait(self, eng, keys):
        return self.add(eng, None, r=keys)

    def emit(self):
        nc = self.nc
        st = self.stack
        esem = {e: st.enter_context(nc.semaphore("s_" + e)) for e in ENGS}
        dsem = {n: st.enter_context(nc.semaphore("d_%d" % i)) for i, n in enumerate(self.dma_cum)}
        ordmap = {}
        for e in ENGS:
            c = 0
            m = {}
            for op in self.ops[e]:
                if op.signal and op.dma_sem is None:
                    c += 1
                    m[op.idx] = c
            ordmap[e] = m

        def run(e, name):
            for op in self.ops[name]:
                for src, v in op.waits.items():
                    if isinstance(src, tuple):
                        e.wait_ge(dsem[src[1]], v)
                    else:
                        e.wait_ge(esem[src], ordmap[src][v])
                if op.fn is None:
                    continue
                ins = op.fn(e)
                if op.dma_sem is not None:
                    ins.then_inc(dsem[op.dma_sem], 16)
                elif op.signal:
                    ins.then_inc(esem[name], 1)

        with nc.Block() as block:
            @block.tensor
            def _(e):
                run(e, "pe")

            @block.scalar
            def _(e):
                run(e, "act")

            @block.vector
            def _(e):
                run(e, "dve")

            @block.gpsimd
            def _(e):
                run(e, "pool")

            @block.sync
            def _(e):
                run(e, "sp")


VC_FFN = 0
VC_MIX = 64
VC_FIN = 96
VC_QN = 104
VC_KVN = 116
VC_HN = 120
NVEC = 124
CC_INVF = 0
CC_SH = 1
CC_TRI = 4
CC_MASK = CC_TRI
CC_ONES = 4 + 512
NCONST = 4 + 512 + 128

GLA_HC = 768
GLA_WIN_COLS = 4 * GLA_HC + 32


def _chunk_cols(v):
    v = np.asarray(v, np.float32)
    return np.ascontiguousarray(v.reshape(-1, 128).T)


def _host_consts():
    c = np.zeros((128, NCONST), np.float32)
    inv = (1.0 / (10000.0 ** (np.arange(0, 64, 2, dtype=np.float32) / np.float32(64)))).astype(np.float32)
    p = np.arange(128)
    c[:, CC_INVF] = inv[p % 32]
    c[:, CC_SH] = np.where(p < 64, 1.5 * math.pi, np.where((p % 64) < 32, 0.0, math.pi))
    s = np.arange(128)[:, None]
    t = np.arange(128)[None, :]
    c[:, CC_TRI:CC_TRI + 128] = (s <= t)
    c[:, CC_TRI + 128:CC_TRI + 256] = (s > t)
    c[:, CC_TRI + 256:CC_TRI + 384] = (s >= t)
    c[:, CC_TRI + 384:CC_TRI + 512] = (s < t)
    c[:, CC_ONES:CC_ONES + 128] = 1.0
    return c


def _host_prepare(inp):
    f = lambda a: np.ascontiguousarray(np.asarray(a, np.float32))
    vec = np.zeros((128, NVEC), np.float32)
    fn = np.asarray(inp["ffn_norm"], np.float32)
    for li in range(DEPTH):
        for w in range(2):
            vec[:, VC_FFN + (li * 2 + w) * 8: VC_FFN + (li * 2 + w) * 8 + 8] = _chunk_cols(fn[li, w])
        vec[:, VC_MIX + li * 8: VC_MIX + li * 8 + 8] = _chunk_cols(np.asarray(inp["mix_norm"])[li])
    vec[:, VC_FIN:VC_FIN + 8] = _chunk_cols(inp["final_norm"])
    for j in range(2):
        vec[:, VC_QN + j * 6: VC_QN + j * 6 + 6] = _chunk_cols(np.asarray(inp["mla_q_norm"])[j])
        vec[:, VC_KVN + j * 2: VC_KVN + j * 2 + 2] = _chunk_cols(np.asarray(inp["mla_kv_norm"])[j])
        vec[:, VC_HN + j * 2: VC_HN + j * 2 + 2] = _chunk_cols(np.asarray(inp["gla_head_norm"])[j])
    gw = np.asarray(inp["gla_w_in"], np.float32)
    cols = []
    for h in range(4):
        cols += list(range(h * 128, (h + 1) * 128))
        cols += list(range(512 + h * 128, 512 + (h + 1) * 128))
        cols += list(range(1024 + h * 256, 1024 + (h + 1) * 256))
        cols += list(range(2048 + h * 256, 2048 + (h + 1) * 256))
    cols += list(range(3072, 3104))
    gla_win = np.ascontiguousarray(gw[:, :, cols])
    g2 = np.concatenate([np.asarray(inp["gla_w_gate2"], np.float32),
                         np.asarray(inp["gla_b_gate"], np.float32)[:, :, None, :]], axis=2)
    g2aug = np.ascontiguousarray(g2.transpose(0, 2, 1, 3).reshape(2, 17, 1024))
    mw = np.asarray(inp["mla_w_in"], np.float32)
    sw = list(range(1024 + 32, 1024 + 64)) + list(range(1024, 1024 + 32))
    mla_win = np.ascontiguousarray(np.concatenate([mw, mw[:, :, [c + 0 for c in sw]]], axis=2))
    uq = np.asarray(inp["mla_w_uq"], np.float32)
    cols = []
    for h in range(8):
        b = h * 192
        cols += list(range(b, b + 192))
        cols += list(range(b + 160, b + 192)) + list(range(b + 128, b + 160))
    mla_wuq = np.ascontiguousarray(uq[:, :, cols])
    shared = {
        "vecs": vec, "consts": _host_consts(),
        "w_gu": f(inp["ffn_w_gu"]).reshape(8, D, 2 * DFF), "w_dn": f(inp["ffn_w_down"]).reshape(8, DFF, D),
        "gla_win": gla_win, "g2aug": g2aug, "gla_wout": f(inp["gla_w_out"]),
        "mla_win": mla_win, "mla_wuq": mla_wuq, "mla_wukv": f(inp["mla_w_ukv"]), "mla_wout": f(inp["mla_w_out"]),
    }
    x = np.asarray(inp["x"], np.float32)
    pos = np.asarray(inp["positions"], np.int32)
    per_core = []
    for b in range(x.shape[0]):
        m = dict(shared)
        m["xT"] = np.ascontiguousarray(x[b].T)
        m["pos"] = np.ascontiguousarray(np.broadcast_to(pos[b][None, :], (128, S)))
        per_core.append(m)
    return per_core


class Ctx:
    pass


def tsl(t):
    return slice(t * TT, (t + 1) * TT)


def build_program(spec, dbg=None):
    nc = bass.Bass("TRN2", target_bir_lowering=False)
    dr = lambda name, shape, dt=F32, kind="ExternalInput": nc.dram_tensor(name, list(shape), dt, kind=kind).ap()
    C = Ctx()
    C.xT_d = dr("xT", [D, S])
    C.pos_d = dr("pos", [128, S], I32)
    C.vecs_d = dr("vecs", [128, NVEC])
    C.consts_d = dr("consts", [128, NCONST])
    C.w_gu = dr("w_gu", [8, D, 2 * DFF])
    C.w_dn = dr("w_dn", [8, DFF, D])
    C.gla_win = dr("gla_win", [2, D, GLA_WIN_COLS])
    C.g2aug = dr("g2aug", [2, 17, 1024])
    C.gla_wout = dr("gla_wout", [2, D, D])
    C.mla_win = dr("mla_win", [2, D, 1152])
    C.mla_wuq = dr("mla_wuq", [2, 768, 2048])
    C.mla_wukv = dr("mla_wukv", [2, 256, 2048])
    C.mla_wout = dr("mla_wout", [2, D, D])
    C.out_d = dr("outT", [D, S], F32, kind="ExternalOutput")
    C.dbg_d = {}
    if dbg:
        for name, shape in dbg.items():
            C.dbg_d[name] = dr("dbg_" + name, shape, F32, kind="ExternalOutput")

    with ExitStack() as st:
        P = Prog(nc, st)
        C.P = P
        C.X = P.sb("X", [128, KC, S], F32)
        C.H = P.sb("H", [128, NT, KC, TT], BF16)
        C.A = P.sb("A", [128, 16384], BF16)
        C.B = P.sb("B", [128, 12288], BF16)
        C.W = [P.sb("W%d" % i, [128, 4096], BF16) for i in range(4)]
        C.R = P.sb("R", [128, S], F32)
        C.vecs = P.sb("vecs", [128, NVEC], F32)
        C.consts = P.sb("consts", [128, 4 + 128], F32)
        C.tri_bf = P.sb("tri_bf", [128, 512], BF16)
        C.ones_bf = P.sb("ones_bf", [128, 128], BF16)
        C.sq = P.sb("sq", [128, 2, TT], BF16)
        C.tmpf = P.sb("tmpf", [128, 3, TT], F32)
        C.rstd = P.sb("rstd", [128, TT], F32)
        C.g2 = P.sb("g2", [128, 1024], BF16)
        C.psb = [P.ps("ps%d" % i, [128, TT]) for i in range(8)]
        C.wslot = 0
        C.psi = 0

        P.dma("sp", C.vecs[:], C.vecs_d, w=["vecs"], sem="vecs")
        P.dma("sp", C.consts[:, 0:4], C.consts_d[:, 0:4], w=[("consts", 0)], sem="consts")
        P.dma("sp", C.consts[:, 4:132], C.consts_d[:, CC_ONES:CC_ONES + 128], w=[("consts", 1)], sem="consts1")
        P.dma("pool", C.tri_bf[:], C.consts_d[:, CC_TRI:CC_TRI + 512], w=["tri_bf"], sem="tri")
        xv = C.xT_d.rearrange("(k p) n -> p k n", p=128)
        for t in range(NT):
            P.dma("sp", C.X[:, :, tsl(t)], xv[:, :, tsl(t)], w=[("X", k, t) for k in range(KC)], sem="x%d" % t)
        P.dve(lambda e: e.memset(C.ones_bf[:], 1.0), w=["ones_bf"])

        C.rope_dirty = True

        def gcol_of(stg):
            if stg[0] == "ffn":
                return VC_FFN + (stg[1] * 2 + stg[2]) * 8
            if stg[0] in ("mla", "gla"):
                return VC_MIX + stg[1] * 8
            return None

        prenormed = False
        for sidx, stg in enumerate(spec):
            nxt = spec[sidx + 1] if sidx + 1 < len(spec) else None
            g_next = gcol_of(nxt) if nxt is not None else None
            if os.environ.get("K_NOHOOK") == "1":
                g_next = None
            hook = (lambda t, bank=None, g=g_next: norm_tile(C, g, t, bank=bank)) if g_next is not None else None
            if stg[0] == "ffn" and nxt is not None and os.environ.get("K_NOHOIST") != "1":
                if nxt[0] == "mla" and C.rope_dirty:
                    P.fence("B")
                    emit_rope_tables(C)
                elif nxt[0] == "gla":
                    emit_gla_setup(C, nxt[1])
            if stg[0] == "ffn":
                emit_ffn(C, stg[1], stg[2], prenormed, hook)
            elif stg[0] == "mla":
                emit_mla(C, stg[1], prenormed, hook)
            elif stg[0] == "gla":
                emit_gla(C, stg[1], prenormed, hook)
            elif stg[0] == "final":
                emit_final(C, True)
            elif stg[0] == "rawout":
                emit_final(C, False)
            prenormed = hook is not None
        P.emit()
    return nc


def psum(C, n=8, base=0):
    b = base + (C.psi % n)
    C.psi += 1
    return b


def wslot(C):
    s = C.wslot % 4
    C.wslot += 1
    return s


def emit_rope_tables(C):
    P = C.P
    C.rope_dirty = False
    two_pi = 2.0 * math.pi
    b0i = C.B[:, 0:2 * S].bitcast(I32)
    b0f = C.B[:, 0:2 * S].bitcast(F32)
    b1f = C.B[:, 2 * S:4 * S].bitcast(F32)
    R = C.R
    P.dma("sp", b0i, C.pos_d, w=[("B", "b0")], sem="pos")
    P.dve(lambda e: e.tensor_copy(out=b1f, in_=b0i), r=[("B", "b0")], w=[("B", "b1")])
    P.dve(lambda e: e.tensor_scalar(out=R[:], in0=b1f, scalar1=C.consts[:, CC_INVF:CC_INVF + 1],
                                    scalar2=C.consts[:, CC_SH:CC_SH + 1], op0=ALU.mult, op1=ALU.add),
          r=[("B", "b1"), "consts"], w=["R"])
    P.dve(lambda e: e.tensor_scalar(out=b0i, in0=R[:], scalar1=1.0 / two_pi, scalar2=None, op0=ALU.mult),
          r=["R"], w=[("B", "b0")])
    P.dve(lambda e: e.tensor_copy(out=b1f, in_=b0i), r=[("B", "b0")], w=[("B", "b1")])
    P.dve(lambda e: e.scalar_tensor_tensor(out=R[:], in0=b1f, scalar=-two_pi, in1=R[:], op0=ALU.mult, op1=ALU.add),
          r=[("B", "b1"), "R"], w=["R"])
    P.dve(lambda e: e.tensor_scalar(out=b0f, in0=R[:], scalar1=0.0, scalar2=two_pi, op0=ALU.is_lt, op1=ALU.mult),
          r=["R"], w=[("B", "b0")])
    P.dve(lambda e: e.tensor_tensor(out=R[:], in0=R[:], in1=b0f, op=ALU.add), r=["R", ("B", "b0")], w=["R"])
    P.dve(lambda e: e.tensor_scalar(out=R[:], in0=R[:], scalar1=-math.pi, scalar2=None, op0=ALU.add),
          r=["R"], w=["R"])
    P.act(lambda e: e.activation(out=R[:], in_=R[:], func=AF.Sin), r=["R"], w=["R"])


def emit_gla_setup(C, li):
    P = C.P
    j = li // 2
    P.fence("R")
    C.rope_dirty = True
    gflat = C.R[:, :].bitcast(BF16)
    P.dve(lambda e: e.memset(C.g2[:], 0.0), w=["g2"])
    P.dma("pool", C.g2[0:17, :], C.g2aug[j], w=["g2"], sem="g2")
    P.dve(lambda e: e.memset(gflat, 0.0), w=[("R",)])
    P.dve(lambda e: e.memset(gflat[0:32, :], 1.0), w=[("R",)])
    C.gla_ready = li


def emit_rmsnorm(C, n_chunks, src_fn, src_keys, gcol, dst_fn, dst_keys, t, inv_n, stats_ps=None, bank=None):
    P = C.P
    if stats_ps is None:
        b = psum(C) if bank is None else bank
        for k in range(n_chunks):
            q = k % 2
            P.act(lambda e, k=k, q=q: e.activation(out=C.sq[:, q, :], in_=src_fn(k), func=AF.Square),
                  r=[src_keys(k)], w=[("sq", q)])
            P.pe(lambda e, k=k, q=q, b=b: e.matmul(C.psb[b][:], lhsT=C.ones_bf[:], rhs=C.sq[:, q, :],
                                                   start=(k == 0), stop=(k == n_chunks - 1)),
                 r=[("sq", q), "ones_bf"], w=[("ps", b)])
    else:
        b = stats_ps
    P.act(lambda e, b=b: e.activation(out=C.tmpf[:, 2, :], in_=C.psb[b][:], func=AF.Ln, bias=EPS, scale=inv_n),
          r=[("ps", b)], w=[("tmpf", 2)])
    P.act(lambda e: e.activation(out=C.rstd[:], in_=C.tmpf[:, 2, :], func=AF.Exp, scale=-0.5), r=[("tmpf", 2)], w=["rstd"])
    for k in range(n_chunks):
        P.dve(lambda e, k=k: e.scalar_tensor_tensor(
            out=dst_fn(k), in0=src_fn(k), scalar=C.vecs[:, gcol + k:gcol + k + 1], in1=C.rstd[:],
            op0=ALU.mult, op1=ALU.mult),
            r=[src_keys(k), "rstd", "vecs"], w=[dst_keys(k)])


def norm_tile(C, gcol, t, bank=None):
    emit_rmsnorm(C, KC, lambda k: C.X[:, k, tsl(t)], lambda k: ("X", k, t), gcol,
                 lambda k: C.H[:, t, k, :], lambda k: ("H", t, k), t, 1.0 / D, bank=bank)


def emit_norm_to_H(C, gcol):
    for t in range(NT):
        norm_tile(C, gcol, t)


def emit_final(C, do_norm):
    P = C.P
    ov = C.out_d.rearrange("(k p) n -> p k n", p=128)
    if not do_norm:
        for t in range(NT):
            P.dma("sp", ov[:, :, tsl(t)], C.X[:, :, tsl(t)], r=[("X", k, t) for k in range(KC)], w=[("out", t)],
                  sem="o%d" % t)
    else:
        P.fence("A")
        P.fence("B")
        for t in range(NT):
            reg, nm = (C.A, "A") if t % 2 == 0 else (C.B, "B")
            of = reg[:, 0:2 * KC * TT].bitcast(F32).rearrange("p (k n) -> p k n", k=KC)
            emit_rmsnorm(C, KC, lambda k, t=t: C.X[:, k, tsl(t)], lambda k, t=t: ("X", k, t), VC_FIN,
                         lambda k, of=of: of[:, k, :], lambda k, nm=nm: (nm, "o", k), t, 1.0 / D)
            P.dma("sp", ov[:, :, tsl(t)], of, r=[(nm, "o")], w=[("out", t)], sem="o%d" % t)
    P.final_wait("sp", [("out", t) for t in range(NT)])


def load_w(C, src_aps, shape_view):
    P = C.P
    s = wslot(C)
    n = 1
    for d in shape_view:
        n *= d
    assert n <= 4096
    flat = C.W[s][:, 0:n]
    if len(shape_view) == 2:
        v = flat.rearrange("p (a b) -> p a b", a=shape_view[0])
    elif len(shape_view) == 3:
        v = flat.rearrange("p (a b c) -> p a b c", a=shape_view[0], b=shape_view[1])
    else:
        v = flat
    if isinstance(src_aps, (list, tuple)):
        keys = []
        for g, src in enumerate(src_aps):
            P.dma("pool", v[:, :, g, :], src, w=[("W", s, g)], sem="w%d_%d" % (s, g))
            keys.append(("W", s, g))
        return v, keys
    P.dma("pool", v, src_aps, w=[("W", s)], sem="w%d_0" % s)
    return v, ("W", s)


def emit_ffn(C, li, which, prenormed=False, hook=None):
    P = C.P
    fi = li * 2 + which
    if not prenormed:
        emit_norm_to_H(C, VC_FFN + fi * 8)
    P.fence("A")
    aT = C.A[:, 0:8 * S].rearrange("p (j n) -> p j n", j=8)
    wgu = C.w_gu[fi].rearrange("(k p) (g n) -> p k g n", p=128, g=2)
    wdn = C.w_dn[fi].rearrange("(j p) n -> p j n", p=128)
    pieces = [(0, 4), (4, 8), (8, 11)]
    for (t0, t1) in pieces:
        nj = (t1 - t0) * 2
        for ti in range(t0, t1):
            wv, wks = load_w(C, [wgu[:, :, g, ti * 256:(ti + 1) * 256] for g in range(2)], (KC, 2, 256))
            for jj in range(2):
                j = (ti - t0) * 2 + jj
                for t in range(NT):
                    bg = psum(C)
                    bu = psum(C)
                    for g, b in ((0, bg), (1, bu)):
                        for k in range(KC):
                            P.pe(lambda e, k=k, g=g, b=b, jj=jj, t=t, wv=wv: e.matmul(
                                C.psb[b][:], lhsT=wv[:, k, g, jj * 128:(jj + 1) * 128], rhs=C.H[:, t, k, :],
                                start=(k == 0), stop=(k == KC - 1)),
                                r=[wks[g], ("H", t, k)], w=[("ps", b)])
                    q = (j * NT + t) % 2
                    P.act(lambda e, bg=bg, q=q: e.activation(out=C.tmpf[:, q, :], in_=C.psb[bg][:], func=AF.Silu),
                          r=[("ps", bg)], w=[("tmpf", q)])
                    P.dve(lambda e, bu=bu, q=q, j=j, t=t: e.tensor_tensor(
                        out=aT[:, j, tsl(t)], in0=C.psb[bu][:], in1=C.tmpf[:, q, :], op=ALU.mult),
                        r=[("ps", bu), ("tmpf", q)], w=[("A", "a", j, t)])
        wds = []
        for c0 in range(0, nj, 4):
            n = min(4, nj - c0)
            r0 = t0 * 2 + c0
            wds.append(load_w(C, wdn[:, r0:r0 + n, :], (n, D)))
        last_piece = (t1 == pieces[-1][1])
        for t in range(NT):
            if last_piece and hook is not None and t >= 1:
                hook(t - 1)
            for i in range(KC):
                b = psum(C)
                for j in range(nj):
                    wv, wk = wds[j // 4]
                    P.pe(lambda e, b=b, j=j, i=i, t=t, wv=wv, nj=nj: e.matmul(
                        C.psb[b][:], lhsT=wv[:, j % 4, i * 128:(i + 1) * 128], rhs=aT[:, j, tsl(t)],
                        start=(j == 0), stop=(j == nj - 1)),
                        r=[wk, ("A", "a", j, t)], w=[("ps", b)])
                P.dve(lambda e, b=b, i=i, t=t: e.scalar_tensor_tensor(
                    out=C.X[:, i, tsl(t)], in0=C.psb[b][:], scalar=0.5, in1=C.X[:, i, tsl(t)],
                    op0=ALU.mult, op1=ALU.add),
                    r=[("ps", b), ("X", i, t)], w=[("X", i, t)])
    if hook is not None:
        hook(NT - 1)


def emit_rope(C, b, dst, dst_key, t):
    P = C.P
    P.dve(lambda e: e.tensor_tensor(out=C.tmpf[0:64, 0, :], in0=C.psb[b][0:64, :], in1=C.R[0:64, tsl(t)], op=ALU.mult),
          r=[("ps", b), ("R", 0)], w=[("tmpf", 0)])
    P.dve(lambda e: e.tensor_tensor(out=C.tmpf[0:64, 1, :], in0=C.psb[b][64:128, :], in1=C.R[64:128, tsl(t)], op=ALU.mult),
          r=[("ps", b), ("R", 64)], w=[("tmpf", 1)])
    P.dve(lambda e: e.tensor_tensor(out=dst, in0=C.tmpf[0:64, 0, :], in1=C.tmpf[0:64, 1, :], op=ALU.add),
          r=[("tmpf", 0), ("tmpf", 1)], w=[dst_key])


def emit_mla(C, li, prenormed=False, hook=None):
    P = C.P
    j = li // 2
    if not prenormed:
        emit_norm_to_H(C, VC_MIX + li * 8)
    P.fence("A")
    P.fence("B")
    if C.rope_dirty:
        emit_rope_tables(C)
        P.fence("B")
    A, B = C.A, C.B
    attn = A[:, :].rearrange("p (h n) -> p h n", h=8)
    ctmps = [A[:, 8192 * i:8192 * (i + 1)].bitcast(F32).rearrange("p (m n) -> p m n", m=8) for i in range(2)]
    qn = B[:, 0:2048]
    qpe = B[:, 2048:4096]
    kn = B[:, 4096:6144]
    vv = B[:, 6144:8192].rearrange("p (c n) -> p c n", c=16)
    kpe = B[:, 8192:10240]
    PT = B[:, 10240:12288].rearrange("p (i n) -> p i n", i=4)
    acc = C.tmpf[:, 0, :]
    ones_f = C.consts[:, 4:132]

    P.dve(lambda e: e.memset(kpe[64:128, :], 0.0), w=[("B", "kpe_pad")])
    P.dve(lambda e: e.memset(qpe[64:128, :], 0.0), w=[("B", "qpe_pad")])

    win = C.mla_win[j].rearrange("(k p) n -> p k n", p=128)
    wt = [load_w(C, win[:, :, 0:512], (KC, 512)), load_w(C, win[:, :, 512:1024], (KC, 512)),
          load_w(C, win[:, :, 1024:1152], (KC, 128))]
    for t in range(NT):
        pend = None
        ctmp = ctmps[t % 2]
        cb = t % 2
        for m in range(9):
            b = psum(C, 4)
            wv, wk = wt[m // 4]
            c0 = (m % 4) * 128
            for k in range(KC):
                P.pe(lambda e, b=b, k=k, wv=wv, c0=c0, t=t: e.matmul(
                    C.psb[b][:], lhsT=wv[:, k, c0:c0 + 128], rhs=C.H[:, t, k, :], start=(k == 0), stop=(k == KC - 1)),
                    r=[wk, ("H", t, k)], w=[("ps", b)])
            if pend is not None:
                pend()
                pend = None
            if m < 8:
                q = m % 2
                P.act(lambda e, b=b, m=m, ctmp=ctmp: e.activation(out=ctmp[:, m, :], in_=C.psb[b][:], func=AF.Copy),
                      r=[("ps", b)], w=[("A", "ctmp", cb, m)])
                P.act(lambda e, b=b, q=q: e.activation(out=C.sq[:, q, :], in_=C.psb[b][:], func=AF.Square),
                      r=[("ps", b)], w=[("sq", q)])
                sb_ = (4 if m < 6 else 5) + 2 * (t % 2)

                def stat(m=m, q=q, sb_=sb_):
                    P.pe(lambda e: e.matmul(C.psb[sb_][:], lhsT=C.ones_bf[:], rhs=C.sq[:, q, :],
                                            start=(m == 0 or m == 6), stop=(m == 5 or m == 7)),
                         r=[("sq", q), "ones_bf"], w=[("ps", sb_)])
                pend = stat
            else:
                emit_rope(C, b, kpe[0:64, tsl(t)], ("B", "kpe", t), t)
        if pend is not None:
            pend()
        emit_rmsnorm(C, 6, lambda k, ctmp=ctmp: ctmp[:, k, :], lambda k, cb=cb: ("A", "ctmp", cb, k), VC_QN + j * 6,
                     lambda k, t=t: C.H[:, t, k, :], lambda k, t=t: ("H", t, k), t, 1.0 / 768, stats_ps=4 + 2 * (t % 2))
        emit_rmsnorm(C, 2, lambda k, ctmp=ctmp: ctmp[:, 6 + k, :], lambda k, cb=cb: ("A", "ctmp", cb, 6 + k), VC_KVN + j * 2,
                     lambda k, t=t: C.H[:, t, 6 + k, :], lambda k, t=t: ("H", t, 6 + k), t, 1.0 / 256, stats_ps=5 + 2 * (t % 2))
    if os.environ.get("MLA_STOP") == "a":
        return
    P.fence("A")
    wukv_d = C.mla_wukv[j].rearrange("(k p) n -> p k n", p=128)
    wuq_d = C.mla_wuq[j].rearrange("(k p) n -> p k n", p=128)
    wout_d = C.mla_wout[j].rearrange("(k p) n -> p k n", p=128)
    sm_scale = 192.0 ** -0.5
    for h in range(8):
        hh = h % 2
        if hh == 0:
            sl = wslot(C)
            wuq = C.W[sl][:, 0:3072].rearrange("p (k n) -> p k n", k=6)
            wukv = C.W[sl][:, 3072:4096].rearrange("p (k n) -> p k n", k=2)
            wuqk, wukvk = ("W", sl, 0), ("W", sl, 1)
            P.dma("pool", wuq, wuq_d[:, :, h * 256:(h + 2) * 256], w=[wuqk], sem="w%d_0" % sl)
            P.dma("pool", wukv, wukv_d[:, :, h * 256:(h + 2) * 256], w=[wukvk], sem="w%d_1" % sl)
        for t in range(NT):
            b = psum(C, 4)
            for k in range(6):
                P.pe(lambda e, b=b, k=k, t=t, hh=hh, wuq=wuq: e.matmul(
                    C.psb[b][:], lhsT=wuq[:, k, hh * 256:hh * 256 + 128], rhs=C.H[:, t, k, :],
                    start=(k == 0), stop=(k == 5)), r=[wuqk, ("H", t, k)], w=[("ps", b)])
            P.act(lambda e, b=b, t=t: e.activation(out=qn[:, tsl(t)], in_=C.psb[b][:], func=AF.Copy),
                  r=[("ps", b)], w=[("B", "qn", t)])
            b = psum(C, 4)
            for k in range(6):
                P.pe(lambda e, b=b, k=k, t=t, hh=hh, wuq=wuq: e.matmul(
                    C.psb[b][:], lhsT=wuq[:, k, hh * 256 + 128:hh * 256 + 256], rhs=C.H[:, t, k, :],
                    start=(k == 0), stop=(k == 5)), r=[wuqk, ("H", t, k)], w=[("ps", b)])
            emit_rope(C, b, qpe[0:64, tsl(t)], ("B", "qpe", t), t)
            b = psum(C, 4)
            for k in range(2):
                P.pe(lambda e, b=b, k=k, t=t, hh=hh, wukv=wukv: e.matmul(
                    C.psb[b][:], lhsT=wukv[:, k, hh * 256:hh * 256 + 128], rhs=C.H[:, t, 6 + k, :],
                    start=(k == 0), stop=(k == 1)), r=[wukvk, ("H", t, 6 + k)], w=[("ps", b)])
            P.act(lambda e, b=b, t=t: e.activation(out=kn[:, tsl(t)], in_=C.psb[b][:], func=AF.Copy),
                  r=[("ps", b)], w=[("B", "kn", t)])
            b = psum(C, 4)
            for c4 in range(4):
                for k in range(2):
                    P.pe(lambda e, b=b, k=k, t=t, hh=hh, c4=c4, wukv=wukv: e.matmul(
                        C.psb[b][:, c4 * 128:(c4 + 1) * 128], lhsT=C.H[:, t, 6 + k, c4 * 128:(c4 + 1) * 128],
                        rhs=wukv[:, k, hh * 256 + 128:hh * 256 + 256], start=(k == 0), stop=(k == 1)),
                        r=[wukvk, ("H", t, 6 + k)], w=[("ps", b)])
            P.act(lambda e, b=b, t=t: e.activation(
                out=vv[:, 4 * t:4 * t + 4, :], in_=C.psb[b][:].rearrange("p (c n) -> p c n", c=4), func=AF.Copy),
                r=[("ps", b)], w=[("B", "v", t)])
        def S(kt, qt):
            b = kt % 3
            P.pe(lambda e: e.matmul(C.psb[b][:], lhsT=kn[:, kt * 128:(kt + 1) * 128], rhs=qn[:, tsl(qt)],
                                    start=True, stop=False),
                 r=[("B", "kn", kt // 4), ("B", "qn", qt)], w=[("ps", b)])
            P.pe(lambda e: e.matmul(C.psb[b][:], lhsT=kpe[:, kt * 128:(kt + 1) * 128], rhs=qpe[:, tsl(qt)],
                                    start=False, stop=True),
                 r=[("B", "kpe", kt // 4), ("B", "qpe", qt), ("B", "kpe_pad"), ("B", "qpe_pad")], w=[("ps", b)])

        accb = C.sq[:, 0, :]
        S(0, 0)
        S(1, 0)
        pending_fin = None
        for qt in range(NT):
            po = 4 + qt % 2
            pd = 6 + qt % 2
            for kt in range(16):
                b = kt % 3
                pi = kt % 4
                P.act(lambda e, b=b, pi=pi: e.activation(out=PT[:, pi, :], in_=C.psb[b][:], func=AF.Exp, scale=sm_scale),
                      r=[("ps", b)], w=[("B", "pt", pi)])
                if kt + 2 < 16:
                    S(kt + 2, qt)
                P.pe(lambda e, kt=kt, pi=pi, po=po: e.matmul(C.psb[po][:], lhsT=vv[:, kt, :], rhs=PT[:, pi, :],
                                                            start=(kt == 0), stop=(kt == 15)),
                     r=[("B", "v", kt // 4), ("B", "pt", pi)], w=[("ps", po)])
                if kt == 2 and pending_fin is not None:
                    pending_fin()
                    pending_fin = None
                if kt in (7, 15):
                    P.pe(lambda e, kt=kt, pi=pi, pd=pd: e.matmul(C.psb[pd][:], lhsT=C.ones_bf[:], rhs=PT[:, pi, :],
                                                                start=(kt == 7), stop=False),
                         r=["ones_bf", ("B", "pt", pi)], w=[("ps", pd)])
                elif kt == 0:
                    P.dve(lambda e, pi=pi: e.tensor_copy(out=acc, in_=PT[:, pi, :]), r=[("B", "pt", pi)], w=[("tmpf", 0)])
                elif kt == 14:
                    P.dve(lambda e, pi=pi: e.tensor_tensor(out=accb, in0=acc, in1=PT[:, pi, :], op=ALU.add),
                          r=[("B", "pt", pi), ("tmpf", 0)], w=[("sq", 0)])
                else:
                    P.dve(lambda e, pi=pi: e.tensor_tensor(out=acc, in0=acc, in1=PT[:, pi, :], op=ALU.add),
                          r=[("B", "pt", pi), ("tmpf", 0)], w=[("tmpf", 0)])
            if qt + 1 < NT:
                S(0, qt + 1)
                S(1, qt + 1)

            def finalize(po=po, pd=pd, qt=qt, h=h):
                P.pe(lambda e: e.matmul(C.psb[pd][:], lhsT=C.ones_bf[:], rhs=accb, start=False, stop=True),
                     r=["ones_bf", ("sq", 0)], w=[("ps", pd)])
                P.act(lambda e: e.activation(out=C.rstd[:], in_=C.psb[pd][:], func=AF.Ln), r=[("ps", pd)], w=["rstd"])
                P.act(lambda e: e.activation(out=C.tmpf[:, 2, :], in_=C.rstd[:], func=AF.Exp, scale=-1.0),
                      r=["rstd"], w=[("tmpf", 2)])
                P.dve(lambda e: e.tensor_tensor(out=attn[:, h, tsl(qt)], in0=C.psb[po][:], in1=C.tmpf[:, 2, :],
                                                op=ALU.mult),
                      r=[("ps", po), ("tmpf", 2)], w=[("A", "attn", h, qt)])
            finalize()
    wos = [load_w(C, wout_d[:, 4 * g:4 * g + 4, :], (4, D)) for g in range(2)]
    for t in range(NT):
        if hook is not None and t >= 1:
            hook(t - 1)
        for i in range(KC):
            b = psum(C, 4)
            for h in range(8):
                wo, wok = wos[h // 4]
                P.pe(lambda e, b=b, i=i, t=t, h=h, wo=wo: e.matmul(
                    C.psb[b][:], lhsT=wo[:, h % 4, i * 128:(i + 1) * 128], rhs=attn[:, h, tsl(t)],
                    start=(h == 0), stop=(h == 7)), r=[wok, ("A", "attn", h, t)], w=[("ps", b)])
            P.dve(lambda e, b=b, i=i, t=t: e.tensor_tensor(out=C.X[:, i, tsl(t)], in0=C.psb[b][:],
                                                           in1=C.X[:, i, tsl(t)], op=ALU.add),
                  r=[("ps", b), ("X", i, t)], w=[("X", i, t)])
    if hook is not None:
        hook(NT - 1)


def emit_gla(C, li, prenormed=False, hook=None):
    P = C.P
    j = li // 2
    if not prenormed:
        emit_norm_to_H(C, VC_MIX + li * 8)
    P.fence("A")
    P.fence("B")
    if getattr(C, "gla_ready", None) != li:
        emit_gla_setup(C, li)
    A, B = C.A, C.B
    qf, kf, qb, kb = A[:, 0:2048], A[:, 2048:4096], A[:, 4096:6144], A[:, 6144:8192]
    vv = A[:, 8192:12288].rearrange("p (c n) -> p c n", c=16)
    keb = A[:, 12288:14336].rearrange("p (c n) -> p c n", c=16)
    ms = A[:, 14336:14848].rearrange("p (i n) -> p i n", i=2)
    kef = A[:, 14848:15104].rearrange("p (i n) -> p i n", i=2)
    on = A[:, 15104:16128].rearrange("p (v n) -> p v n", v=2)
    dec = A[:, 16128:16192].bitcast(F32).rearrange("p (c d) -> p c d", c=16)
    Sf = B[:, 0:4096].rearrange("p (c n) -> p c n", c=16)
    Sb = B[:, 4096:8192].rearrange("p (c n) -> p c n", c=16)
    stf = B[:, 8192:9216].bitcast(F32).rearrange("p (d n) -> p d n", d=2)
    la = B[:, 9216:9728].rearrange("p (i n) -> p i n", i=2)
    otmp = B[:, 10240:11264].bitcast(F32)
    ktok = otmp.rearrange("p (i n) -> p i n", i=4)
    e1 = B[:, 11264:11776].bitcast(F32)
    ee = B[:, 11776:12288].bitcast(F32)
    gflat = C.R[:, :].bitcast(BF16)
    gaug = gflat.rearrange("p (d n) -> p d n", d=2)
    tri = lambda i: C.tri_bf[:, i * 128:(i + 1) * 128]
    dk_scale = 128.0 ** -0.5
    NS = -1.0 / 16.0

    win = C.gla_win[j].rearrange("(k p) n -> p k n", p=128)
    wo_d = C.gla_wout[j].rearrange("(k p) n -> p k n", p=128)
    wg, wgk = load_w(C, win[:, :, 4 * GLA_HC:4 * GLA_HC + 32], (KC, 32))
    for t in range(NT):
        for d in range(2):
            b = 6 + d
            for k in range(KC):
                P.pe(lambda e, b=b, k=k, d=d, t=t: e.matmul(
                    C.psb[b][0:16, :], lhsT=wg[:, k, d * 16:(d + 1) * 16], rhs=C.H[:, t, k, :],
                    start=(k == 0), stop=(k == KC - 1)), r=[wgk, ("H", t, k)], w=[("ps", b)])
            P.act(lambda e, b=b, d=d, t=t: e.activation(out=gaug[0:16, d, tsl(t)], in_=C.psb[b][0:16, :], func=AF.Copy),
                  r=[("ps", b)], w=[("R", "g", d, t)])

    for h in range(4):
        w1, w1k = load_w(C, win[:, :, h * GLA_HC:h * GLA_HC + 512], (KC, 512))
        w2, w2k = load_w(C, win[:, :, h * GLA_HC + 512:h * GLA_HC + 768], (KC, 256))
        w3, w3k = load_w(C, wo_d[:, 2 * h:2 * h + 2, :], (2, D))
        P.dve(lambda e: e.memset(stf[:, 0, :], 0.0), w=[("B", "st", 0)])

        def proj(t, w1=w1, w1k=w1k):
            for b, c0 in ((0, 0), (1, 128)):
                for k in range(KC):
                    P.pe(lambda e, b=b, c0=c0, k=k: e.matmul(
                        C.psb[b][:], lhsT=w1[:, k, c0:c0 + 128], rhs=C.H[:, t, k, :],
                        start=(k == 0), stop=(k == KC - 1)), r=[w1k, ("H", t, k)], w=[("ps", b)])

        def stA(c, w1=w1, w1k=w1k, h=h):
            t, c4 = c // 4, c % 4
            lb = c % 2
            bt = 2 + lb
            cs = slice(c4 * 128, (c4 + 1) * 128)
            for k in range(KC):
                P.pe(lambda e, k=k: e.matmul(
                    C.psb[bt][:, 0:384], lhsT=C.H[:, t, k, cs], rhs=w1[:, k, 128:512],
                    start=(k == 0), stop=(k == KC - 1)), r=[w1k, ("H", t, k)], w=[("ps", bt)])
            for d in range(2):
                P.pe(lambda e, d=d: e.matmul(
                    C.psb[6][:, d * 128:(d + 1) * 128], lhsT=gaug[:, d, c * 128:(c + 1) * 128],
                    rhs=C.g2[:, d * 512 + h * 128:d * 512 + (h + 1) * 128], start=True, stop=True),
                    r=["g2", ("R", "g", d, c // 4)], w=[("ps", 6)])
            P.act(lambda e: e.activation(out=e1, in_=C.psb[6][:, 0:256], func=AF.Exp, scale=-1.0),
                  r=[("ps", 6)], w=[("B", "e1")])
            P.act(lambda e: e.activation(out=la[:, lb, :], in_=e1, func=AF.Ln, bias=1.0),
                  r=[("B", "e1")], w=[("B", "la", lb)])
            P.act(lambda e: e.activation(out=vv[:, c, :], in_=C.psb[bt][:, 128:384], func=AF.Copy),
                  r=[("ps", bt)], w=[("A", "v", c)])
            P.dve(lambda e: e.tensor_copy(out=ktok[:, c % 4, :], in_=C.psb[bt][:, 0:128]),
                  r=[("ps", bt)], w=[("B", "otmp", c % 4)])

        def stB(c):
            c4 = c % 4
            lb = c % 2
            bt = 2 + lb
            cs = slice(c4 * 128, (c4 + 1) * 128)
            P.pe(lambda e: e.matmul(C.psb[4][:, cs], lhsT=la[:, lb, 0:128], rhs=tri(0), start=True, stop=True),
                 r=[("B", "la", lb), "tri_bf"], w=[("ps", 4)])
            P.pe(lambda e: e.matmul(C.psb[5][:, cs], lhsT=la[:, lb, 128:256], rhs=tri(2), start=True, stop=True),
                 r=[("B", "la", lb), "tri_bf"], w=[("ps", 5)])
            P.pe(lambda e: e.matmul(C.psb[6][:, 256:384], lhsT=tri(1), rhs=la[:, lb, 0:128], start=True, stop=True),
                 r=[("B", "la", lb), "tri_bf"], w=[("ps", 6)])
            P.pe(lambda e: e.matmul(C.psb[6][:, 384:512], lhsT=tri(3), rhs=la[:, lb, 128:256], start=True, stop=True),
                 r=[("B", "la", lb), "tri_bf"], w=[("ps", 6)])
            P.act(lambda e: e.activation(out=ee, in_=C.psb[6][:, 256:512], func=AF.Exp, scale=NS),
                  r=[("ps", 6)], w=[("B", "ee")])
            P.dve(lambda e: e.tensor_tensor(out=kef[:, lb, :], in0=ktok[:, c % 4, :], in1=ee[:, 0:128], op=ALU.mult),
                  r=[("B", "otmp", c % 4), ("B", "ee")], w=[("A", "kef", lb)])
            P.dve(lambda e: e.tensor_tensor(out=keb[:, c, :], in0=ktok[:, c % 4, :], in1=ee[:, 128:256], op=ALU.mult),
                  r=[("B", "otmp", c % 4), ("B", "ee")], w=[("A", "keb", c)])
            P.act(lambda e: e.activation(out=dec[:, c, 0:1], in_=C.psb[4][:, c4 * 128 + 127:c4 * 128 + 128],
                                         func=AF.Exp, scale=NS), r=[("ps", 4)], w=[("A", "dec", c, 0)])
            P.act(lambda e: e.activation(out=dec[:, c, 1:2], in_=C.psb[5][:, c4 * 128:c4 * 128 + 1],
                                         func=AF.Exp, scale=NS), r=[("ps", 5)], w=[("A", "dec", c, 1)])

        cur = [0]

        def stC(c):
            lb = c % 2
            a, b2 = cur[0], 1 - cur[0]
            cur[0] = b2
            P.dve(lambda e: e.tensor_copy(out=Sf[:, c, :], in_=stf[:, a, :]),
                  r=[("B", "st", a)], w=[("B", "Sf", c)])
            P.pe(lambda e: e.matmul(C.psb[7][:, 0:256], lhsT=kef[:, lb, :], rhs=vv[:, c, :], start=True, stop=True),
                 r=[("A", "kef", lb), ("A", "v", c)], w=[("ps", 7)])
            P.dve(lambda e: e.scalar_tensor_tensor(out=stf[:, b2, :], in0=stf[:, a, :], scalar=dec[:, c, 0:1],
                                                   in1=C.psb[7][:, 0:256], op0=ALU.mult, op1=ALU.add),
                  r=[("B", "st", a), ("A", "dec", c, 0), ("ps", 7)], w=[("B", "st", b2)])

        def tile_end(t):
            tmps = [(C.tmpf[:, 0, :], ("tmpf", 0)), (C.tmpf[:, 1, :], ("tmpf", 1)), (C.tmpf[:, 2, :], ("tmpf", 2)),
                    (C.rstd[:], ("rstd",))]
            jobs = [(4, NS, 0, qf, "qf"), (4, -NS, 1, kf, "kf"), (5, NS, 0, qb, "qb"), (5, -NS, 1, kb, "kb")]
            for (bb, sc, src, dst, nm), (tb, tk) in zip(jobs, tmps):
                P.act(lambda e, bb=bb, sc=sc, tb=tb: e.activation(out=tb, in_=C.psb[bb][:], func=AF.Exp, scale=sc),
                      r=[("ps", bb)], w=[tk])
            for (bb, sc, src, dst, nm), (tb, tk) in zip(jobs, tmps):
                if src == 0:
                    P.dve(lambda e, dst=dst, tb=tb: e.scalar_tensor_tensor(
                        out=dst[:, tsl(t)], in0=C.psb[0][:], scalar=dk_scale, in1=tb, op0=ALU.mult, op1=ALU.mult),
                        r=[("ps", 0), tk], w=[("A", nm, t)])
                else:
                    P.dve(lambda e, dst=dst, tb=tb: e.tensor_tensor(out=dst[:, tsl(t)], in0=C.psb[1][:], in1=tb,
                                                                    op=ALU.mult),
                          r=[("ps", 1), tk], w=[("A", nm, t)])

        proj(0)
        for step in range(16 + 3):
            if 0 <= step - 2 < 16:
                stB(step - 2)
                if (step - 2) % 4 == 3:
                    tile_end((step - 2) // 4)
            if 0 <= step - 3 < 16:
                stC(step - 3)
            if step < 16:
                stA(step)
            if step >= 7 and (step - 7) % 4 == 0 and (step - 7) // 4 + 1 < NT:
                proj((step - 7) // 4 + 1)

        order = list(reversed(range(16)))

        def kvb(i):
            c = order[i]
            bk = 6 + i % 2
            P.pe(lambda e: e.matmul(C.psb[bk][:, 0:256], lhsT=keb[:, c, :], rhs=vv[:, c, :], start=True, stop=True),
                 r=[("A", "keb", c), ("A", "v", c)], w=[("ps", bk)])
        kvb(0)
        P.dve(lambda e: e.memset(stf[:, 0, :], 0.0), w=[("B", "st", 0)])
        cur[0] = 0
        for i, c in enumerate(order):
            bk = 6 + i % 2
            a, b2 = cur[0], 1 - cur[0]
            cur[0] = b2
            P.act(lambda e, c=c, a=a: e.activation(out=Sb[:, c, :], in_=stf[:, a, :], func=AF.Copy),
                  r=[("B", "st", a)], w=[("B", "Sb", c)])
            if i + 1 < 16:
                kvb(i + 1)
            P.dve(lambda e, c=c, bk=bk, a=a, b2=b2: e.scalar_tensor_tensor(
                out=stf[:, b2, :], in0=stf[:, a, :], scalar=dec[:, c, 1:2], in1=C.psb[bk][:, 0:256],
                op0=ALU.mult, op1=ALU.add),
                r=[("B", "st", a), ("A", "dec", c, 1), ("ps", bk)], w=[("B", "st", b2)])

        def chunks(t, extras=()):
            extras = list(extras)
            pob = (1, 2) if t % 2 == 0 else (5, 6)

            def scores(c):
                sb_ = 0 if c % 2 == 0 else 7
                cs = slice(c * 128, (c + 1) * 128)
                P.pe(lambda e: e.matmul(C.psb[sb_][:, 0:128], lhsT=kf[:, cs], rhs=qf[:, cs], start=True, stop=True),
                     r=[("A", "kf", t), ("A", "qf", t)], w=[("ps", sb_)])
                P.pe(lambda e: e.matmul(C.psb[sb_][:, 128:256], lhsT=kb[:, cs], rhs=qb[:, cs], start=True, stop=True),
                     r=[("A", "kb", t), ("A", "qb", t)], w=[("ps", sb_)])
            scores(4 * t)
            for c4 in range(4):
                c = 4 * t + c4
                mb = c % 2
                sb_ = 0 if c % 2 == 0 else 7
                cs = slice(c * 128, (c + 1) * 128)
                if c4 + 1 < 4:
                    scores(c + 1)
                P.dve(lambda e, mb=mb, sb_=sb_: e.tensor_tensor(out=ms[:, mb, :], in0=C.psb[sb_][:, 0:256],
                                                                in1=C.tri_bf[:, 0:256], op=ALU.mult),
                      r=[("ps", sb_), "tri_bf"], w=[("A", "ms", mb)])
                for vc in range(2):
                    ob = pob[vc]
                    osl = slice(c4 * 128, (c4 + 1) * 128)
                    vs = slice(vc * 128, (vc + 1) * 128)
                    P.pe(lambda e, ob=ob, osl=osl, vs=vs, c=c, mb=mb: e.matmul(
                        C.psb[ob][:, osl], lhsT=vv[:, c, vs], rhs=ms[:, mb, 0:128], start=True, stop=False),
                        r=[("A", "v", c), ("A", "ms", mb)], w=[("ps", ob)])
                    P.pe(lambda e, ob=ob, osl=osl, vs=vs, c=c, mb=mb: e.matmul(
                        C.psb[ob][:, osl], lhsT=vv[:, c, vs], rhs=ms[:, mb, 128:256], start=False, stop=False),
                        r=[("A", "v", c), ("A", "ms", mb)], w=[("ps", ob)])
                    P.pe(lambda e, ob=ob, osl=osl, vs=vs, c=c, cs=cs: e.matmul(
                        C.psb[ob][:, osl], lhsT=Sf[:, c, vs], rhs=qf[:, cs], start=False, stop=False),
                        r=[("B", "Sf", c), ("A", "qf", t)], w=[("ps", ob)])
                    P.pe(lambda e, ob=ob, osl=osl, vs=vs, c=c, cs=cs: e.matmul(
                        C.psb[ob][:, osl], lhsT=Sb[:, c, vs], rhs=qb[:, cs], start=False, stop=True),
                        r=[("B", "Sb", c), ("A", "qb", t)], w=[("ps", ob)])
                for _ in range(2):
                    if extras:
                        extras.pop(0)()
            while extras:
                extras.pop(0)()

        def R_pieces(t, w2=w2, w2k=w2k):
            out = []
            for vc in range(2):
                b = 3 + vc
                for k0 in range(0, KC, 2):
                    def mm(k0=k0, vc=vc, b=b):
                        for k in (k0, k0 + 1):
                            P.pe(lambda e, k=k: e.matmul(
                                C.psb[b][:], lhsT=w2[:, k, vc * 128:(vc + 1) * 128], rhs=C.H[:, t, k, :],
                                start=(k == 0), stop=(k == KC - 1)), r=[w2k, ("H", t, k)], w=[("ps", b)])
                    out.append(mm)
                out.append(lambda vc=vc, b=b: P.act(
                    lambda e: e.activation(out=C.tmpf[:, vc, :], in_=C.psb[b][:], func=AF.Silu),
                    r=[("ps", b)], w=[("tmpf", vc)]))
            return out

        otmp2 = B[:, 10240:12288].bitcast(F32).rearrange("p (v n) -> p v n", v=2)
        o2keys = ([("B", "otmp")], [("B", "e1"), ("B", "ee")])

        def N_a(t):
            pob = (1, 2) if t % 2 == 0 else (5, 6)
            for vc in range(2):
                P.act(lambda e, vc=vc: e.activation(out=C.sq[:, vc, :], in_=C.psb[pob[vc]][:], func=AF.Square),
                      r=[("ps", pob[vc])], w=[("sq", vc)])
                P.pe(lambda e, vc=vc: e.matmul(C.psb[3][:], lhsT=C.ones_bf[:], rhs=C.sq[:, vc, :],
                                               start=(vc == 0), stop=(vc == 1)),
                     r=[("sq", vc), "ones_bf"], w=[("ps", 3)])
            for vc in range(2):
                P.dve(lambda e, vc=vc: e.tensor_tensor(out=otmp2[:, vc, :], in0=C.psb[pob[vc]][:], in1=C.tmpf[:, vc, :],
                                                       op=ALU.mult),
                      r=[("ps", pob[vc]), ("tmpf", vc)], w=o2keys[vc])

        def N_b(t):
            P.act(lambda e: e.activation(out=C.tmpf[:, 2, :], in_=C.psb[3][:], func=AF.Ln, bias=EPS, scale=1.0 / 256),
                  r=[("ps", 3)], w=[("tmpf", 2)])
            P.act(lambda e: e.activation(out=C.rstd[:], in_=C.tmpf[:, 2, :], func=AF.Exp, scale=-0.5),
                  r=[("tmpf", 2)], w=["rstd"])
            for vc in range(2):
                gc = VC_HN + j * 2 + vc
                P.dve(lambda e, vc=vc, gc=gc: e.scalar_tensor_tensor(
                    out=on[:, vc, :], in0=otmp2[:, vc, :], scalar=C.vecs[:, gc:gc + 1], in1=C.rstd[:],
                    op0=ALU.mult, op1=ALU.mult), r=o2keys[vc] + ["vecs", "rstd"], w=[("A", "on", vc)])

        def W_pieces(t, w3=w3, w3k=w3k):
            def piece(i):
                b = 0 if i % 2 == 0 else 7
                for vc in range(2):
                    P.pe(lambda e, vc=vc: e.matmul(
                        C.psb[b][:], lhsT=w3[:, vc, i * 128:(i + 1) * 128], rhs=on[:, vc, :],
                        start=(vc == 0), stop=(vc == 1)), r=[w3k, ("A", "on", vc)], w=[("ps", b)])
                P.dve(lambda e: e.tensor_tensor(out=C.X[:, i, tsl(t)], in0=C.psb[b][:], in1=C.X[:, i, tsl(t)], op=ALU.add),
                      r=[("ps", b), ("X", i, t)], w=[("X", i, t)])
            return [lambda i=i: piece(i) for i in range(KC)]

        for pc in R_pieces(0):
            pc()
        chunks(0)
        chunks(1)
        for t in range(NT):
            N_a(t)
            if t + 2 < NT:
                chunks(t + 2)
            N_b(t)
            wp = W_pieces(t)
            rp = R_pieces(t + 1) if t + 1 < NT else []
            for _ in range(min(5, len(rp))):
                rp.pop(0)()
            while wp or rp:
                if wp:
                    wp.pop(0)()
                if rp:
                    rp.pop(0)()
            if h == 3 and hook is not None:
                hook(t, bank=3)


FULL_SPEC = []
for _li in range(DEPTH):
    FULL_SPEC.append(("ffn", _li, 0))
    FULL_SPEC.append(("gla", _li) if _li % 2 == 0 else ("mla", _li))
    FULL_SPEC.append(("ffn", _li, 1))
FULL_SPEC.append(("final",))


def kernel(**inputs):
    per_core = _host_prepare(inputs)
    nc = build_program(FULL_SPEC)
    res = run_bass_kernel_spmd(nc, per_core, core_ids=list(range(len(per_core))))
    out = np.stack([np.ascontiguousarray(r["outT"].T) for r in res.results], axis=0)
    return out.astype(np.float32)
```

```python
from collections import defaultdict
from contextlib import ExitStack
import math
import os
import numpy as np
import concourse.bass as bass
import concourse.mybir as mybir
from concourse.bass_utils import run_bass_kernel_spmd

F32 = mybir.dt.float32
BF16 = mybir.dt.bfloat16
I32 = mybir.dt.int32
ALU = mybir.AluOpType
AF = mybir.ActivationFunctionType

ENGS = ("pe", "act", "dve", "pool", "sp")

D = 1024
S = 2048
DEPTH = 4
DFF = 2816
NT = 4
TT = 512
KC = 8
EPS = 1e-6


class Op:
    __slots__ = ("eng", "fn", "waits", "signal", "idx", "dma_sem", "dma_val", "ordinal")

    def __init__(self, eng, fn, idx):
        self.eng = eng
        self.fn = fn
        self.idx = idx
        self.waits = {}
        self.signal = False
        self.dma_sem = None
        self.dma_val = 0
        self.ordinal = 0


class Prog:
    def __init__(self, nc, stack):
        self.nc = nc
        self.stack = stack
        self.ops = {e: [] for e in ENGS}
        self.lastw = {}
        self.readers = {}
        self.bufkeys = defaultdict(set)
        self.seen = {e: {} for e in ENGS}
        self.dma_cum = {}
        self.raw_window = 2
        self.full_same_engine_sync = os.environ.get("K_FULLSYNC", "1") == "1"

    def sb(self, name, shape, dt):
        return self.stack.enter_context(self.nc.sbuf_tensor("sb_" + name, list(shape), dt))

    def ps(self, name, shape, dt=F32):
        return self.stack.enter_context(self.nc.psum_tensor("pp_" + name, list(shape), dt))

    def _conf(self, key):
        n = len(key)
        for k2 in self.bufkeys.get(key[0], ()):
            m = len(k2)
            if m <= n:
                if key[:m] == k2:
                    yield k2
            elif k2[:n] == key:
                yield k2

    def _need(self, op, d, kind):
        if d is op:
            return
        if d.dma_sem is not None:
            src = ("dma", d.dma_sem)
            val = d.dma_val
        else:
            src = d.eng
            val = d.idx
            if d.eng == op.eng:
                if op.eng == "pe" or op.eng == "sp":
                    return
                if not self.full_same_engine_sync:
                    if kind != "raw":
                        return
                    if op.idx - d.idx > self.raw_window:
                        return
        if self.seen[op.eng].get(src, -1) >= val:
            return
        self.seen[op.eng][src] = val
        op.waits[src] = val
        d.signal = True

    def add(self, eng, fn, r=(), w=(), dma_sem=None):
        op = Op(eng, fn, len(self.ops[eng]))
        if dma_sem is not None:
            op.dma_sem = dma_sem
            self.dma_cum[dma_sem] = self.dma_cum.get(dma_sem, 0) + 16
            op.dma_val = self.dma_cum[dma_sem]
        r = [tuple(k) if isinstance(k, (tuple, list)) else (k,) for k in r]
        w = [tuple(k) if isinstance(k, (tuple, list)) else (k,) for k in w]
        for k in r:
            for k2 in self._conf(k):
                d = self.lastw.get(k2)
                if d is not None:
                    self._need(op, d, "raw")
                if k[0] == "ps":
                    for d in self.readers.get(k2, {}).values():
                        if d.eng != eng:
                            self._need(op, d, "rar")
        for k in w:
            for k2 in list(self._conf(k)):
                d = self.lastw.get(k2)
                if d is not None:
                    self._need(op, d, "waw")
                for d in self.readers.get(k2, {}).values():
                    self._need(op, d, "war")
        rid = eng if dma_sem is None else ("dma", dma_sem)
        for k in r:
            self.bufkeys[k[0]].add(k)
            self.readers.setdefault(k, {})[rid] = op
        for k in w:
            n = len(k)
            for k2 in list(self._conf(k)):
                if len(k2) >= n:
                    self.lastw[k2] = op
                    self.readers[k2] = {}
            self.bufkeys[k[0]].add(k)
            self.lastw[k] = op
            self.readers[k] = {}
        self.ops[eng].append(op)
        return op

    def pe(self, fn, r=(), w=()):
        return self.add("pe", fn, r, w)

    def act(self, fn, r=(), w=()):
        return self.add("act", fn, r, w)

    def dve(self, fn, r=(), w=()):
        return self.add("dve", fn, r, w)

    def pool(self, fn, r=(), w=()):
        return self.add("pool", fn, r, w)

    def dma(self, q, out, in_, r=(), w=(), sem=None, **kw):
        return self.add(q, lambda e: e.dma_start(out=out, in_=in_, **kw), r, w, dma_sem=sem)

    def fence(self, region):
        return self.add("sp", lambda e: e.nop(), r=(), w=[(region,)])

    def final_wait(self, eng, keys):
        return self.add(eng, None, r=keys)

    def emit(self):
        nc = self.nc
        st = self.stack
        EPOCH = 3000
        ordmap = {}
        nsem = {}
        for e in ENGS:
            c = 0
            m = {}
            for op in self.ops[e]:
                if op.signal and op.dma_sem is None:
                    c += 1
                    m[op.idx] = c
            ordmap[e] = m
            nsem[e] = max(1, (c + EPOCH - 1) // EPOCH)
        esem = {e: [st.enter_context(nc.semaphore("s_%s%d" % (e, i))) for i in range(nsem[e])] for e in ENGS}
        dsem = {n: st.enter_context(nc.semaphore("d_%d" % i)) for i, n in enumerate(self.dma_cum)}

        def run(e, name):
            for op in self.ops[name]:
                for src, v in op.waits.items():
                    if isinstance(src, tuple):
                        e.wait_ge(dsem[src[1]], v)
                    else:
                        o = ordmap[src][v] - 1
                        e.wait_ge(esem[src][o // EPOCH], o % EPOCH + 1)
                if op.fn is None:
                    continue
                ins = op.fn(e)
                if op.dma_sem is not None:
                    ins.then_inc(dsem[op.dma_sem], 16)
                elif op.signal:
                    o = ordmap[name][op.idx] - 1
                    ins.then_inc(esem[name][o // EPOCH], 1)

        with nc.Block() as block:
            @block.tensor
            def _(e):
                run(e, "pe")

            @block.scalar
            def _(e):
                run(e, "act")

            @block.vector
            def _(e):
                run(e, "dve")

            @block.gpsimd
            def _(e):
                run(e, "pool")

            @block.sync
            def _(e):
                run(e, "sp")


VC_FFN = 0
VC_MIX = 64
VC_FIN = 96
VC_QN = 104
VC_KVN = 116
VC_HN = 120
NVEC = 124
CC_INVF = 0
CC_SH = 1
CC_TRI = 4
CC_MASK = CC_TRI
CC_ONES = 4 + 512
NCONST = 4 + 512 + 128

GLA_HC = 768
GLA_WIN_COLS = 4 * GLA_HC + 32


def _chunk_cols(v):
    v = np.asarray(v, np.float32)
    return np.ascontiguousarray(v.reshape(-1, 128).T)


def _host_consts():
    c = np.zeros((128, NCONST), np.float32)
    inv = (1.0 / (10000.0 ** (np.arange(0, 64, 2, dtype=np.float32) / np.float32(64)))).astype(np.float32)
    p = np.arange(128)
    c[:, CC_INVF] = inv[p % 32]
    c[:, CC_SH] = np.where(p < 64, 1.5 * math.pi, np.where((p % 64) < 32, 0.0, math.pi))
    s = np.arange(128)[:, None]
    t = np.arange(128)[None, :]
    c[:, CC_TRI:CC_TRI + 128] = (s <= t)
    c[:, CC_TRI + 128:CC_TRI + 256] = (s > t)
    c[:, CC_TRI + 256:CC_TRI + 384] = (s >= t)
    c[:, CC_TRI + 384:CC_TRI + 512] = (s < t)
    c[:, CC_ONES:CC_ONES + 128] = 1.0
    return c


def _host_prepare(inp):
    f = lambda a: np.ascontiguousarray(np.asarray(a, np.float32))
    vec = np.zeros((128, NVEC), np.float32)
    fn = np.asarray(inp["ffn_norm"], np.float32)
    for li in range(DEPTH):
        for w in range(2):
            vec[:, VC_FFN + (li * 2 + w) * 8: VC_FFN + (li * 2 + w) * 8 + 8] = _chunk_cols(fn[li, w])
        vec[:, VC_MIX + li * 8: VC_MIX + li * 8 + 8] = _chunk_cols(np.asarray(inp["mix_norm"])[li])
    vec[:, VC_FIN:VC_FIN + 8] = _chunk_cols(inp["final_norm"])
    for j in range(2):
        vec[:, VC_QN + j * 6: VC_QN + j * 6 + 6] = _chunk_cols(np.asarray(inp["mla_q_norm"])[j])
        vec[:, VC_KVN + j * 2: VC_KVN + j * 2 + 2] = _chunk_cols(np.asarray(inp["mla_kv_norm"])[j])
        vec[:, VC_HN + j * 2: VC_HN + j * 2 + 2] = _chunk_cols(np.asarray(inp["gla_head_norm"])[j])
    gw = np.asarray(inp["gla_w_in"], np.float32)
    cols = []
    for h in range(4):
        cols += list(range(h * 128, (h + 1) * 128))
        cols += list(range(512 + h * 128, 512 + (h + 1) * 128))
        cols += list(range(1024 + h * 256, 1024 + (h + 1) * 256))
        cols += list(range(2048 + h * 256, 2048 + (h + 1) * 256))
    cols += list(range(3072, 3104))
    gla_win = np.ascontiguousarray(gw[:, :, cols])
    g2 = np.concatenate([np.asarray(inp["gla_w_gate2"], np.float32),
                         np.asarray(inp["gla_b_gate"], np.float32)[:, :, None, :]], axis=2)
    g2aug = np.ascontiguousarray(g2.transpose(0, 2, 1, 3).reshape(2, 17, 1024))
    mw = np.asarray(inp["mla_w_in"], np.float32)
    sw = list(range(1024 + 32, 1024 + 64)) + list(range(1024, 1024 + 32))
    mla_win = np.ascontiguousarray(np.concatenate([mw, mw[:, :, [c + 0 for c in sw]]], axis=2))
    uq = np.asarray(inp["mla_w_uq"], np.float32)
    cols = []
    for h in range(8):
        b = h * 192
        cols += list(range(b, b + 192))
        cols += list(range(b + 160, b + 192)) + list(range(b + 128, b + 160))
    mla_wuq = np.ascontiguousarray(uq[:, :, cols])
    shared = {
        "vecs": vec, "consts": _host_consts(),
        "w_gu": f(inp["ffn_w_gu"]).reshape(8, D, 2 * DFF), "w_dn": f(inp["ffn_w_down"]).reshape(8, DFF, D),
        "gla_win": gla_win, "g2aug": g2aug, "gla_wout": f(inp["gla_w_out"]),
        "mla_win": mla_win, "mla_wuq": mla_wuq, "mla_wukv": f(inp["mla_w_ukv"]), "mla_wout": f(inp["mla_w_out"]),
    }
    x = np.asarray(inp["x"], np.float32)
    pos = np.asarray(inp["positions"], np.int32)
    per_core = []
    for b in range(x.shape[0]):
        m = dict(shared)
        m["xT"] = np.ascontiguousarray(x[b].T)
        m["pos"] = np.ascontiguousarray(np.broadcast_to(pos[b][None, :], (128, S)))
        per_core.append(m)
    return per_core


class Ctx:
    pass


def tsl(t):
    return slice(t * TT, (t + 1) * TT)


def build_program(spec, dbg=None):
    nc = bass.Bass("TRN2", target_bir_lowering=False)
    dr = lambda name, shape, dt=F32, kind="ExternalInput": nc.dram_tensor(name, list(shape), dt, kind=kind).ap()
    C = Ctx()
    C.xT_d = dr("xT", [D, S])
    C.pos_d = dr("pos", [128, S], I32)
    C.vecs_d = dr("vecs", [128, NVEC])
    C.consts_d = dr("consts", [128, NCONST])
    C.w_gu = dr("w_gu", [8, D, 2 * DFF])
    C.w_dn = dr("w_dn", [8, DFF, D])
    C.gla_win = dr("gla_win", [2, D, GLA_WIN_COLS])
    C.g2aug = dr("g2aug", [2, 17, 1024])
    C.gla_wout = dr("gla_wout", [2, D, D])
    C.mla_win = dr("mla_win", [2, D, 1152])
    C.mla_wuq = dr("mla_wuq", [2, 768, 2048])
    C.mla_wukv = dr("mla_wukv", [2, 256, 2048])
    C.mla_wout = dr("mla_wout", [2, D, D])
    C.out_d = dr("outT", [D, S], F32, kind="ExternalOutput")
    C.dbg_d = {}
    if dbg:
        for name, shape in dbg.items():
            C.dbg_d[name] = dr("dbg_" + name, shape, F32, kind="ExternalOutput")

    with ExitStack() as st:
        P = Prog(nc, st)
        C.P = P
        C.X = P.sb("X", [128, KC, S], F32)
        C.H = P.sb("H", [128, NT, KC, TT], BF16)
        C.A = P.sb("A", [128, 16384], BF16)
        C.B = P.sb("B", [128, 12288], BF16)
        C.W = [P.sb("W%d" % i, [128, 4096], BF16) for i in range(4)]
        C.R = P.sb("R", [128, S], F32)
        C.vecs = P.sb("vecs", [128, NVEC], F32)
        C.consts = P.sb("consts", [128, 4 + 128], F32)
        C.tri_bf = P.sb("tri_bf", [128, 512], BF16)
        C.ones_bf = P.sb("ones_bf", [128, 128], BF16)
        C.sq = P.sb("sq", [128, 2, TT], BF16)
        C.tmpf = P.sb("tmpf", [128, 3, TT], F32)
        C.rstd = P.sb("rstd", [128, TT], F32)
        C.g2 = P.sb("g2", [128, 1024], BF16)
        C.psb = [P.ps("ps%d" % i, [128, TT]) for i in range(8)]
        C.wslot = 0
        C.psi = 0

        P.dma("sp", C.vecs[:], C.vecs_d, w=["vecs"], sem="vecs")
        P.dma("sp", C.consts[:, 0:4], C.consts_d[:, 0:4], w=[("consts", 0)], sem="consts")
        P.dma("sp", C.consts[:, 4:132], C.consts_d[:, CC_ONES:CC_ONES + 128], w=[("consts", 1)], sem="consts1")
        P.dma("pool", C.tri_bf[:], C.consts_d[:, CC_TRI:CC_TRI + 512], w=["tri_bf"], sem="tri")
        xv = C.xT_d.rearrange("(k p) n -> p k n", p=128)
        for t in range(NT):
            P.dma("sp", C.X[:, :, tsl(t)], xv[:, :, tsl(t)], w=[("X", k, t) for k in range(KC)], sem="x%d" % t)
        P.dve(lambda e: e.memset(C.ones_bf[:], 1.0), w=["ones_bf"])

        C.rope_dirty = True

        def gcol_of(stg):
            if stg[0] == "ffn":
                return VC_FFN + (stg[1] * 2 + stg[2]) * 8
            if stg[0] in ("mla", "gla"):
                return VC_MIX + stg[1] * 8
            return None

        prenormed = False
        for sidx, stg in enumerate(spec):
            nxt = spec[sidx + 1] if sidx + 1 < len(spec) else None
            g_next = gcol_of(nxt) if nxt is not None else None
            if os.environ.get("K_NOHOOK") == "1":
                g_next = None
            hook = (lambda t, bank=None, g=g_next: norm_tile(C, g, t, bank=bank)) if g_next is not None else None
            if stg[0] == "ffn" and nxt is not None and os.environ.get("K_NOHOIST") != "1":
                if nxt[0] == "mla" and C.rope_dirty:
                    P.fence("B")
                    emit_rope_tables(C)
                elif nxt[0] == "gla":
                    emit_gla_setup(C, nxt[1])
            if stg[0] == "ffn":
                emit_ffn(C, stg[1], stg[2], prenormed, hook)
            elif stg[0] == "mla":
                emit_mla(C, stg[1], prenormed, hook)
            elif stg[0] == "gla":
                emit_gla(C, stg[1], prenormed, hook)
            elif stg[0] == "final":
                emit_final(C, True)
            elif stg[0] == "rawout":
                emit_final(C, False)
            prenormed = hook is not None
        P.emit()
    return nc


def psum(C, n=8, base=0):
    b = base + (C.psi % n)
    C.psi += 1
    return b


def wslot(C):
    s = C.wslot % 4
    C.wslot += 1
    return s


def emit_rope_tables(C):
    P = C.P
    C.rope_dirty = False
    two_pi = 2.0 * math.pi
    b0i = C.B[:, 0:2 * S].bitcast(I32)
    b0f = C.B[:, 0:2 * S].bitcast(F32)
    b1f = C.B[:, 2 * S:4 * S].bitcast(F32)
    R = C.R
    P.dma("sp", b0i, C.pos_d, w=[("B", "b0")], sem="pos")
    P.dve(lambda e: e.tensor_copy(out=b1f, in_=b0i), r=[("B", "b0")], w=[("B", "b1")])
    P.dve(lambda e: e.tensor_scalar(out=R[:], in0=b1f, scalar1=C.consts[:, CC_INVF:CC_INVF + 1],
                                    scalar2=C.consts[:, CC_SH:CC_SH + 1], op0=ALU.mult, op1=ALU.add),
          r=[("B", "b1"), "consts"], w=["R"])
    P.dve(lambda e: e.tensor_scalar(out=b0i, in0=R[:], scalar1=1.0 / two_pi, scalar2=None, op0=ALU.mult),
          r=["R"], w=[("B", "b0")])
    P.dve(lambda e: e.tensor_copy(out=b1f, in_=b0i), r=[("B", "b0")], w=[("B", "b1")])
    P.dve(lambda e: e.scalar_tensor_tensor(out=R[:], in0=b1f, scalar=-two_pi, in1=R[:], op0=ALU.mult, op1=ALU.add),
          r=[("B", "b1"), "R"], w=["R"])
    P.dve(lambda e: e.tensor_scalar(out=b0f, in0=R[:], scalar1=0.0, scalar2=two_pi, op0=ALU.is_lt, op1=ALU.mult),
          r=["R"], w=[("B", "b0")])
    P.dve(lambda e: e.tensor_tensor(out=R[:], in0=R[:], in1=b0f, op=ALU.add), r=["R", ("B", "b0")], w=["R"])
    P.dve(lambda e: e.tensor_scalar(out=R[:], in0=R[:], scalar1=-math.pi, scalar2=None, op0=ALU.add),
          r=["R"], w=["R"])
    P.act(lambda e: e.activation(out=R[:], in_=R[:], func=AF.Sin), r=["R"], w=["R"])


def emit_gla_setup(C, li):
    P = C.P
    j = li // 2
    P.fence("R")
    C.rope_dirty = True
    gflat = C.R[:, :].bitcast(BF16)
    P.dve(lambda e: e.memset(C.g2[:], 0.0), w=["g2"])
    P.dma("pool", C.g2[0:17, :], C.g2aug[j], w=["g2"], sem="g2")
    P.dve(lambda e: e.memset(gflat, 0.0), w=[("R",)])
    P.dve(lambda e: e.memset(gflat[0:32, :], 1.0), w=[("R",)])
    C.gla_ready = li


def emit_rmsnorm(C, n_chunks, src_fn, src_keys, gcol, dst_fn, dst_keys, t, inv_n, stats_ps=None, bank=None):
    P = C.P
    if stats_ps is None:
        b = psum(C) if bank is None else bank
        for k in range(n_chunks):
            q = k % 2
            P.act(lambda e, k=k, q=q: e.activation(out=C.sq[:, q, :], in_=src_fn(k), func=AF.Square),
                  r=[src_keys(k)], w=[("sq", q)])
            P.pe(lambda e, k=k, q=q, b=b: e.matmul(C.psb[b][:], lhsT=C.ones_bf[:], rhs=C.sq[:, q, :],
                                                   start=(k == 0), stop=(k == n_chunks - 1)),
                 r=[("sq", q), "ones_bf"], w=[("ps", b)])
    else:
        b = stats_ps
    P.act(lambda e, b=b: e.activation(out=C.tmpf[:, 2, :], in_=C.psb[b][:], func=AF.Ln, bias=EPS, scale=inv_n),
          r=[("ps", b)], w=[("tmpf", 2)])
    P.act(lambda e: e.activation(out=C.rstd[:], in_=C.tmpf[:, 2, :], func=AF.Exp, scale=-0.5), r=[("tmpf", 2)], w=["rstd"])
    for k in range(n_chunks):
        P.dve(lambda e, k=k: e.scalar_tensor_tensor(
            out=dst_fn(k), in0=src_fn(k), scalar=C.vecs[:, gcol + k:gcol + k + 1], in1=C.rstd[:],
            op0=ALU.mult, op1=ALU.mult),
            r=[src_keys(k), "rstd", "vecs"], w=[dst_keys(k)])


def norm_tile(C, gcol, t, bank=None):
    emit_rmsnorm(C, KC, lambda k: C.X[:, k, tsl(t)], lambda k: ("X", k, t), gcol,
                 lambda k: C.H[:, t, k, :], lambda k: ("H", t, k), t, 1.0 / D, bank=bank)


def emit_norm_to_H(C, gcol):
    for t in range(NT):
        norm_tile(C, gcol, t)


def emit_final(C, do_norm):
    P = C.P
    ov = C.out_d.rearrange("(k p) n -> p k n", p=128)
    if not do_norm:
        for t in range(NT):
            P.dma("sp", ov[:, :, tsl(t)], C.X[:, :, tsl(t)], r=[("X", k, t) for k in range(KC)], w=[("out", t)],
                  sem="o%d" % t)
    else:
        P.fence("A")
        P.fence("B")
        for t in range(NT):
            reg, nm = (C.A, "A") if t % 2 == 0 else (C.B, "B")
            of = reg[:, 0:2 * KC * TT].bitcast(F32).rearrange("p (k n) -> p k n", k=KC)
            emit_rmsnorm(C, KC, lambda k, t=t: C.X[:, k, tsl(t)], lambda k, t=t: ("X", k, t), VC_FIN,
                         lambda k, of=of: of[:, k, :], lambda k, nm=nm: (nm, "o", k), t, 1.0 / D)
            P.dma("sp", ov[:, :, tsl(t)], of, r=[(nm, "o")], w=[("out", t)], sem="o%d" % t)
    P.final_wait("sp", [("out", t) for t in range(NT)])


def load_w(C, src_aps, shape_view):
    P = C.P
    s = wslot(C)
    n = 1
    for d in shape_view:
        n *= d
    assert n <= 4096
    flat = C.W[s][:, 0:n]
    if len(shape_view) == 2:
        v = flat.rearrange("p (a b) -> p a b", a=shape_view[0])
    elif len(shape_view) == 3:
        v = flat.rearrange("p (a b c) -> p a b c", a=shape_view[0], b=shape_view[1])
    else:
        v = flat
    if isinstance(src_aps, (list, tuple)):
        keys = []
        for g, src in enumerate(src_aps):
            P.dma("pool", v[:, :, g, :], src, w=[("W", s, g)], sem="w%d_%d" % (s, g))
            keys.append(("W", s, g))
        return v, keys
    P.dma("pool", v, src_aps, w=[("W", s)], sem="w%d_0" % s)
    return v, ("W", s)


def emit_ffn(C, li, which, prenormed=False, hook=None):
    P = C.P
    fi = li * 2 + which
    if not prenormed:
        emit_norm_to_H(C, VC_FFN + fi * 8)
    P.fence("A")
    aT = C.A[:, 0:8 * S].rearrange("p (j n) -> p j n", j=8)
    wgu = C.w_gu[fi].rearrange("(k p) (g n) -> p k g n", p=128, g=2)
    wdn = C.w_dn[fi].rearrange("(j p) n -> p j n", p=128)
    pieces = [(0, 4), (4, 8), (8, 11)]
    for (t0, t1) in pieces:
        nj = (t1 - t0) * 2
        for ti in range(t0, t1):
            wv, wks = load_w(C, [wgu[:, :, g, ti * 256:(ti + 1) * 256] for g in range(2)], (KC, 2, 256))
            for jj in range(2):
                j = (ti - t0) * 2 + jj
                for t in range(NT):
                    bg = psum(C)
                    bu = psum(C)
                    for g, b in ((0, bg), (1, bu)):
                        for k in range(KC):
                            P.pe(lambda e, k=k, g=g, b=b, jj=jj, t=t, wv=wv: e.matmul(
                                C.psb[b][:], lhsT=wv[:, k, g, jj * 128:(jj + 1) * 128], rhs=C.H[:, t, k, :],
                                start=(k == 0), stop=(k == KC - 1)),
                                r=[wks[g], ("H", t, k)], w=[("ps", b)])
                    q = (j * NT + t) % 2
                    P.act(lambda e, bg=bg, q=q: e.activation(out=C.tmpf[:, q, :], in_=C.psb[bg][:], func=AF.Silu),
                          r=[("ps", bg)], w=[("tmpf", q)])
                    P.dve(lambda e, bu=bu, q=q, j=j, t=t: e.tensor_tensor(
                        out=aT[:, j, tsl(t)], in0=C.psb[bu][:], in1=C.tmpf[:, q, :], op=ALU.mult),
                        r=[("ps", bu), ("tmpf", q)], w=[("A", "a", j, t)])
        wds = []
        for c0 in range(0, nj, 4):
            n = min(4, nj - c0)
            r0 = t0 * 2 + c0
            wds.append(load_w(C, wdn[:, r0:r0 + n, :], (n, D)))
        last_piece = (t1 == pieces[-1][1])
        for t in range(NT):
            if last_piece and hook is not None and t >= 1:
                hook(t - 1)
            for i in range(KC):
                b = psum(C)
                for j in range(nj):
                    wv, wk = wds[j // 4]
                    P.pe(lambda e, b=b, j=j, i=i, t=t, wv=wv, nj=nj: e.matmul(
                        C.psb[b][:], lhsT=wv[:, j % 4, i * 128:(i + 1) * 128], rhs=aT[:, j, tsl(t)],
                        start=(j == 0), stop=(j == nj - 1)),
                        r=[wk, ("A", "a", j, t)], w=[("ps", b)])
                P.dve(lambda e, b=b, i=i, t=t: e.scalar_tensor_tensor(
                    out=C.X[:, i, tsl(t)], in0=C.psb[b][:], scalar=0.5, in1=C.X[:, i, tsl(t)],
                    op0=ALU.mult, op1=ALU.add),
                    r=[("ps", b), ("X", i, t)], w=[("X", i, t)])
    if hook is not None:
        hook(NT - 1)


def emit_rope(C, b, dst, dst_key, t):
    P = C.P
    P.dve(lambda e: e.tensor_tensor(out=C.tmpf[0:64, 0, :], in0=C.psb[b][0:64, :], in1=C.R[0:64, tsl(t)], op=ALU.mult),
          r=[("ps", b), ("R", 0)], w=[("tmpf", 0)])
    P.dve(lambda e: e.tensor_tensor(out=C.tmpf[0:64, 1, :], in0=C.psb[b][64:128, :], in1=C.R[64:128, tsl(t)], op=ALU.mult),
          r=[("ps", b), ("R", 64)], w=[("tmpf", 1)])
    P.dve(lambda e: e.tensor_tensor(out=dst, in0=C.tmpf[0:64, 0, :], in1=C.tmpf[0:64, 1, :], op=ALU.add),
          r=[("tmpf", 0), ("tmpf", 1)], w=[dst_key])


def emit_mla(C, li, prenormed=False, hook=None):
    P = C.P
    j = li // 2
    if not prenormed:
        emit_norm_to_H(C, VC_MIX + li * 8)
    P.fence("A")
    P.fence("B")
    if C.rope_dirty:
        emit_rope_tables(C)
        P.fence("B")
    A, B = C.A, C.B
    attn = A[:, :].rearrange("p (h n) -> p h n", h=8)
    ctmps = [A[:, 8192 * i:8192 * (i + 1)].bitcast(F32).rearrange("p (m n) -> p m n", m=8) for i in range(2)]
    qn = B[:, 0:2048]
    qpe = B[:, 2048:4096]
    kn = B[:, 4096:6144]
    vv = B[:, 6144:8192].rearrange("p (c n) -> p c n", c=16)
    kpe = B[:, 8192:10240]
    PT = B[:, 10240:12288].rearrange("p (i n) -> p i n", i=4)
    acc = C.tmpf[:, 0, :]
    ones_f = C.consts[:, 4:132]

    P.dve(lambda e: e.memset(kpe[64:128, :], 0.0), w=[("B", "kpe_pad")])
    P.dve(lambda e: e.memset(qpe[64:128, :], 0.0), w=[("B", "qpe_pad")])

    win = C.mla_win[j].rearrange("(k p) n -> p k n", p=128)
    wt = [load_w(C, win[:, :, 0:512], (KC, 512)), load_w(C, win[:, :, 512:1024], (KC, 512)),
          load_w(C, win[:, :, 1024:1152], (KC, 128))]
    for t in range(NT):
        pend = None
        ctmp = ctmps[t % 2]
        cb = t % 2
        for m in range(9):
            b = psum(C, 4)
            wv, wk = wt[m // 4]
            c0 = (m % 4) * 128
            for k in range(KC):
                P.pe(lambda e, b=b, k=k, wv=wv, c0=c0, t=t: e.matmul(
                    C.psb[b][:], lhsT=wv[:, k, c0:c0 + 128], rhs=C.H[:, t, k, :], start=(k == 0), stop=(k == KC - 1)),
                    r=[wk, ("H", t, k)], w=[("ps", b)])
            if pend is not None:
                pend()
                pend = None
            if m < 8:
                q = m % 2
                P.act(lambda e, b=b, m=m, ctmp=ctmp: e.activation(out=ctmp[:, m, :], in_=C.psb[b][:], func=AF.Copy),
                      r=[("ps", b)], w=[("A", "ctmp", cb, m)])
                P.act(lambda e, b=b, q=q: e.activation(out=C.sq[:, q, :], in_=C.psb[b][:], func=AF.Square),
                      r=[("ps", b)], w=[("sq", q)])
                sb_ = (4 if m < 6 else 5) + 2 * (t % 2)

                def stat(m=m, q=q, sb_=sb_):
                    P.pe(lambda e: e.matmul(C.psb[sb_][:], lhsT=C.ones_bf[:], rhs=C.sq[:, q, :],
                                            start=(m == 0 or m == 6), stop=(m == 5 or m == 7)),
                         r=[("sq", q), "ones_bf"], w=[("ps", sb_)])
                pend = stat
            else:
                emit_rope(C, b, kpe[0:64, tsl(t)], ("B", "kpe", t), t)
        if pend is not None:
            pend()
        emit_rmsnorm(C, 6, lambda k, ctmp=ctmp: ctmp[:, k, :], lambda k, cb=cb: ("A", "ctmp", cb, k), VC_QN + j * 6,
                     lambda k, t=t: C.H[:, t, k, :], lambda k, t=t: ("H", t, k), t, 1.0 / 768, stats_ps=4 + 2 * (t % 2))
        emit_rmsnorm(C, 2, lambda k, ctmp=ctmp: ctmp[:, 6 + k, :], lambda k, cb=cb: ("A", "ctmp", cb, 6 + k), VC_KVN + j * 2,
                     lambda k, t=t: C.H[:, t, 6 + k, :], lambda k, t=t: ("H", t, 6 + k), t, 1.0 / 256, stats_ps=5 + 2 * (t % 2))
    if os.environ.get("MLA_STOP") == "a":
        return
    P.fence("A")
    wukv_d = C.mla_wukv[j].rearrange("(k p) n -> p k n", p=128)
    wuq_d = C.mla_wuq[j].rearrange("(k p) n -> p k n", p=128)
    wout_d = C.mla_wout[j].rearrange("(k p) n -> p k n", p=128)
    sm_scale = 192.0 ** -0.5
    for h in range(8):
        hh = h % 2
        if hh == 0:
            sl = wslot(C)
            wuq = C.W[sl][:, 0:3072].rearrange("p (k n) -> p k n", k=6)
            wukv = C.W[sl][:, 3072:4096].rearrange("p (k n) -> p k n", k=2)
            wuqk, wukvk = ("W", sl, 0), ("W", sl, 1)
            P.dma("pool", wuq, wuq_d[:, :, h * 256:(h + 2) * 256], w=[wuqk], sem="w%d_0" % sl)
            P.dma("pool", wukv, wukv_d[:, :, h * 256:(h + 2) * 256], w=[wukvk], sem="w%d_1" % sl)
        for t in range(NT):
            b = psum(C, 4)
            for k in range(6):
                P.pe(lambda e, b=b, k=k, t=t, hh=hh, wuq=wuq: e.matmul(
                    C.psb[b][:], lhsT=wuq[:, k, hh * 256:hh * 256 + 128], rhs=C.H[:, t, k, :],
                    start=(k == 0), stop=(k == 5)), r=[wuqk, ("H", t, k)], w=[("ps", b)])
            P.act(lambda e, b=b, t=t: e.activation(out=qn[:, tsl(t)], in_=C.psb[b][:], func=AF.Copy),
                  r=[("ps", b)], w=[("B", "qn", t)])
            b = psum(C, 4)
            for k in range(6):
                P.pe(lambda e, b=b, k=k, t=t, hh=hh, wuq=wuq: e.matmul(
                    C.psb[b][:], lhsT=wuq[:, k, hh * 256 + 128:hh * 256 + 256], rhs=C.H[:, t, k, :],
                    start=(k == 0), stop=(k == 5)), r=[wuqk, ("H", t, k)], w=[("ps", b)])
            emit_rope(C, b, qpe[0:64, tsl(t)], ("B", "qpe", t), t)
            b = psum(C, 4)
            for k in range(2):
                P.pe(lambda e, b=b, k=k, t=t, hh=hh, wukv=wukv: e.matmul(
                    C.psb[b][:], lhsT=wukv[:, k, hh * 256:hh * 256 + 128], rhs=C.H[:, t, 6 + k, :],
                    start=(k == 0), stop=(k == 1)), r=[wukvk, ("H", t, 6 + k)], w=[("ps", b)])
            P.act(lambda e, b=b, t=t: e.activation(out=kn[:, tsl(t)], in_=C.psb[b][:], func=AF.Copy),
                  r=[("ps", b)], w=[("B", "kn", t)])
            b = psum(C, 4)
            for c4 in range(4):
                for k in range(2):
                    P.pe(lambda e, b=b, k=k, t=t, hh=hh, c4=c4, wukv=wukv: e.matmul(
                        C.psb[b][:, c4 * 128:(c4 + 1) * 128], lhsT=C.H[:, t, 6 + k, c4 * 128:(c4 + 1) * 128],
                        rhs=wukv[:, k, hh * 256 + 128:hh * 256 + 256], start=(k == 0), stop=(k == 1)),
                        r=[wukvk, ("H", t, 6 + k)], w=[("ps", b)])
            P.act(lambda e, b=b, t=t: e.activation(
                out=vv[:, 4 * t:4 * t + 4, :], in_=C.psb[b][:].rearrange("p (c n) -> p c n", c=4), func=AF.Copy),
                r=[("ps", b)], w=[("B", "v", t)])
        def S(kt, qt):
            b = kt % 3
            P.pe(lambda e: e.matmul(C.psb[b][:], lhsT=kn[:, kt * 128:(kt + 1) * 128], rhs=qn[:, tsl(qt)],
                                    start=True, stop=False),
                 r=[("B", "kn", kt // 4), ("B", "qn", qt)], w=[("ps", b)])
            P.pe(lambda e: e.matmul(C.psb[b][:], lhsT=kpe[:, kt * 128:(kt + 1) * 128], rhs=qpe[:, tsl(qt)],
                                    start=False, stop=True),
                 r=[("B", "kpe", kt // 4), ("B", "qpe", qt), ("B", "kpe_pad"), ("B", "qpe_pad")], w=[("ps", b)])

        accb = C.sq[:, 0, :]
        S(0, 0)
        S(1, 0)
        pending_fin = None
        for qt in range(NT):
            po = 4 + qt % 2
            pd = 6 + qt % 2
            for kt in range(16):
                b = kt % 3
                pi = kt % 4
                P.act(lambda e, b=b, pi=pi: e.activation(out=PT[:, pi, :], in_=C.psb[b][:], func=AF.Exp, scale=sm_scale),
                      r=[("ps", b)], w=[("B", "pt", pi)])
                if kt + 2 < 16:
                    S(kt + 2, qt)
                P.pe(lambda e, kt=kt, pi=pi, po=po: e.matmul(C.psb[po][:], lhsT=vv[:, kt, :], rhs=PT[:, pi, :],
                                                            start=(kt == 0), stop=(kt == 15)),
                     r=[("B", "v", kt // 4), ("B", "pt", pi)], w=[("ps", po)])
                if kt == 2 and pending_fin is not None:
                    pending_fin()
                    pending_fin = None
                if kt in (7, 15):
                    P.pe(lambda e, kt=kt, pi=pi, pd=pd: e.matmul(C.psb[pd][:], lhsT=C.ones_bf[:], rhs=PT[:, pi, :],
                                                                start=(kt == 7), stop=False),
                         r=["ones_bf", ("B", "pt", pi)], w=[("ps", pd)])
                elif kt == 0:
                    P.dve(lambda e, pi=pi: e.tensor_copy(out=acc, in_=PT[:, pi, :]), r=[("B", "pt", pi)], w=[("tmpf", 0)])
                elif kt == 14:
                    P.dve(lambda e, pi=pi: e.tensor_tensor(out=accb, in0=acc, in1=PT[:, pi, :], op=ALU.add),
                          r=[("B", "pt", pi), ("tmpf", 0)], w=[("sq", 0)])
                else:
                    P.dve(lambda e, pi=pi: e.tensor_tensor(out=acc, in0=acc, in1=PT[:, pi, :], op=ALU.add),
                          r=[("B", "pt", pi), ("tmpf", 0)], w=[("tmpf", 0)])
            if qt + 1 < NT:
                S(0, qt + 1)
                S(1, qt + 1)

            def finalize(po=po, pd=pd, qt=qt, h=h):
                P.pe(lambda e: e.matmul(C.psb[pd][:], lhsT=C.ones_bf[:], rhs=accb, start=False, stop=True),
                     r=["ones_bf", ("sq", 0)], w=[("ps", pd)])
                P.act(lambda e: e.activation(out=C.rstd[:], in_=C.psb[pd][:], func=AF.Ln), r=[("ps", pd)], w=["rstd"])
                P.act(lambda e: e.activation(out=C.tmpf[:, 2, :], in_=C.rstd[:], func=AF.Exp, scale=-1.0),
                      r=["rstd"], w=[("tmpf", 2)])
                P.dve(lambda e: e.tensor_tensor(out=attn[:, h, tsl(qt)], in0=C.psb[po][:], in1=C.tmpf[:, 2, :],
                                                op=ALU.mult),
                      r=[("ps", po), ("tmpf", 2)], w=[("A", "attn", h, qt)])
            finalize()
    wos = [load_w(C, wout_d[:, 4 * g:4 * g + 4, :], (4, D)) for g in range(2)]
    for t in range(NT):
        if hook is not None and t >= 1:
            hook(t - 1)
        for i in range(KC):
            b = psum(C, 4)
            for h in range(8):
                wo, wok = wos[h // 4]
                P.pe(lambda e, b=b, i=i, t=t, h=h, wo=wo: e.matmul(
                    C.psb[b][:], lhsT=wo[:, h % 4, i * 128:(i + 1) * 128], rhs=attn[:, h, tsl(t)],
                    start=(h == 0), stop=(h == 7)), r=[wok, ("A", "attn", h, t)], w=[("ps", b)])
            P.dve(lambda e, b=b, i=i, t=t: e.tensor_tensor(out=C.X[:, i, tsl(t)], in0=C.psb[b][:],
                                                           in1=C.X[:, i, tsl(t)], op=ALU.add),
                  r=[("ps", b), ("X", i, t)], w=[("X", i, t)])
    if hook is not None:
        hook(NT - 1)


def emit_gla(C, li, prenormed=False, hook=None):
    P = C.P
    j = li // 2
    if not prenormed:
        emit_norm_to_H(C, VC_MIX + li * 8)
    P.fence("A")
    P.fence("B")
    if getattr(C, "gla_ready", None) != li:
        emit_gla_setup(C, li)
    A, B = C.A, C.B
    qf, kf, qb, kb = A[:, 0:2048], A[:, 2048:4096], A[:, 4096:6144], A[:, 6144:8192]
    vv = A[:, 8192:12288].rearrange("p (c n) -> p c n", c=16)
    keb = A[:, 12288:14336].rearrange("p (c n) -> p c n", c=16)
    ms = A[:, 14336:14848].rearrange("p (i n) -> p i n", i=2)
    kef = A[:, 14848:15104].rearrange("p (i n) -> p i n", i=2)
    on = A[:, 15104:16128].rearrange("p (v n) -> p v n", v=2)
    dec = A[:, 16128:16192].bitcast(F32).rearrange("p (c d) -> p c d", c=16)
    Sf = B[:, 0:4096].rearrange("p (c n) -> p c n", c=16)
    Sb = B[:, 4096:8192].rearrange("p (c n) -> p c n", c=16)
    stf = B[:, 8192:9216].bitcast(F32).rearrange("p (d n) -> p d n", d=2)
    la = B[:, 9216:9728].rearrange("p (i n) -> p i n", i=2)
    otmp = B[:, 10240:11264].bitcast(F32)
    ktok = otmp.rearrange("p (i n) -> p i n", i=4)
    e1 = B[:, 11264:11776].bitcast(F32)
    ee = B[:, 11776:12288].bitcast(F32)
    gflat = C.R[:, :].bitcast(BF16)
    gaug = gflat.rearrange("p (d n) -> p d n", d=2)
    tri = lambda i: C.tri_bf[:, i * 128:(i + 1) * 128]
    dk_scale = 128.0 ** -0.5
    NS = -1.0 / 16.0

    win = C.gla_win[j].rearrange("(k p) n -> p k n", p=128)
    wo_d = C.gla_wout[j].rearrange("(k p) n -> p k n", p=128)
    wg, wgk = load_w(C, win[:, :, 4 * GLA_HC:4 * GLA_HC + 32], (KC, 32))
    for t in range(NT):
        for d in range(2):
            b = 6 + d
            for k in range(KC):
                P.pe(lambda e, b=b, k=k, d=d, t=t: e.matmul(
                    C.psb[b][0:16, :], lhsT=wg[:, k, d * 16:(d + 1) * 16], rhs=C.H[:, t, k, :],
                    start=(k == 0), stop=(k == KC - 1)), r=[wgk, ("H", t, k)], w=[("ps", b)])
            P.act(lambda e, b=b, d=d, t=t: e.activation(out=gaug[0:16, d, tsl(t)], in_=C.psb[b][0:16, :], func=AF.Copy),
                  r=[("ps", b)], w=[("R", "g", d, t)])

    for h in range(4):
        w1, w1k = load_w(C, win[:, :, h * GLA_HC:h * GLA_HC + 512], (KC, 512))
        w2, w2k = load_w(C, win[:, :, h * GLA_HC + 512:h * GLA_HC + 768], (KC, 256))
        w3, w3k = load_w(C, wo_d[:, 2 * h:2 * h + 2, :], (2, D))
        P.dve(lambda e: e.memset(stf[:, 0, :], 0.0), w=[("B", "st", 0)])

        def proj(t, w1=w1, w1k=w1k):
            for b, c0 in ((0, 0), (1, 128)):
                for k in range(KC):
                    P.pe(lambda e, b=b, c0=c0, k=k: e.matmul(
                        C.psb[b][:], lhsT=w1[:, k, c0:c0 + 128], rhs=C.H[:, t, k, :],
                        start=(k == 0), stop=(k == KC - 1)), r=[w1k, ("H", t, k)], w=[("ps", b)])

        def stA(c, w1=w1, w1k=w1k, h=h):
            t, c4 = c // 4, c % 4
            lb = c % 2
            bt = 2 + lb
            cs = slice(c4 * 128, (c4 + 1) * 128)
            for k in range(KC):
                P.pe(lambda e, k=k: e.matmul(
                    C.psb[bt][:, 0:384], lhsT=C.H[:, t, k, cs], rhs=w1[:, k, 128:512],
                    start=(k == 0), stop=(k == KC - 1)), r=[w1k, ("H", t, k)], w=[("ps", bt)])
            for d in range(2):
                P.pe(lambda e, d=d: e.matmul(
                    C.psb[6][:, d * 128:(d + 1) * 128], lhsT=gaug[:, d, c * 128:(c + 1) * 128],
                    rhs=C.g2[:, d * 512 + h * 128:d * 512 + (h + 1) * 128], start=True, stop=True),
                    r=["g2", ("R", "g", d, c // 4)], w=[("ps", 6)])
            P.act(lambda e: e.activation(out=e1, in_=C.psb[6][:, 0:256], func=AF.Exp, scale=-1.0),
                  r=[("ps", 6)], w=[("B", "e1")])
            P.act(lambda e: e.activation(out=la[:, lb, :], in_=e1, func=AF.Ln, bias=1.0),
                  r=[("B", "e1")], w=[("B", "la", lb)])
            P.act(lambda e: e.activation(out=vv[:, c, :], in_=C.psb[bt][:, 128:384], func=AF.Copy),
                  r=[("ps", bt)], w=[("A", "v", c)])
            P.dve(lambda e: e.tensor_copy(out=ktok[:, c % 4, :], in_=C.psb[bt][:, 0:128]),
                  r=[("ps", bt)], w=[("B", "otmp", c % 4)])

        def stB(c):
            c4 = c % 4
            lb = c % 2
            bt = 2 + lb
            cs = slice(c4 * 128, (c4 + 1) * 128)
            P.pe(lambda e: e.matmul(C.psb[4][:, cs], lhsT=la[:, lb, 0:128], rhs=tri(0), start=True, stop=True),
                 r=[("B", "la", lb), "tri_bf"], w=[("ps", 4)])
            P.pe(lambda e: e.matmul(C.psb[5][:, cs], lhsT=la[:, lb, 128:256], rhs=tri(2), start=True, stop=True),
                 r=[("B", "la", lb), "tri_bf"], w=[("ps", 5)])
            P.pe(lambda e: e.matmul(C.psb[6][:, 256:384], lhsT=tri(1), rhs=la[:, lb, 0:128], start=True, stop=True),
                 r=[("B", "la", lb), "tri_bf"], w=[("ps", 6)])
            P.pe(lambda e: e.matmul(C.psb[6][:, 384:512], lhsT=tri(3), rhs=la[:, lb, 128:256], start=True, stop=True),
                 r=[("B", "la", lb), "tri_bf"], w=[("ps", 6)])
            P.act(lambda e: e.activation(out=ee, in_=C.psb[6][:, 256:512], func=AF.Exp, scale=NS),
                  r=[("ps", 6)], w=[("B", "ee")])
            P.dve(lambda e: e.tensor_tensor(out=kef[:, lb, :], in0=ktok[:, c % 4, :], in1=ee[:, 0:128], op=ALU.mult),
                  r=[("B", "otmp", c % 4), ("B", "ee")], w=[("A", "kef", lb)])
            P.dve(lambda e: e.tensor_tensor(out=keb[:, c, :], in0=ktok[:, c % 4, :], in1=ee[:, 128:256], op=ALU.mult),
                  r=[("B", "otmp", c % 4), ("B", "ee")], w=[("A", "keb", c)])
            P.act(lambda e: e.activation(out=dec[:, c, 0:1], in_=C.psb[4][:, c4 * 128 + 127:c4 * 128 + 128],
                                         func=AF.Exp, scale=NS), r=[("ps", 4)], w=[("A", "dec", c, 0)])
            P.act(lambda e: e.activation(out=dec[:, c, 1:2], in_=C.psb[5][:, c4 * 128:c4 * 128 + 1],
                                         func=AF.Exp, scale=NS), r=[("ps", 5)], w=[("A", "dec", c, 1)])

        cur = [0]

        def stC(c):
            lb = c % 2
            a, b2 = cur[0], 1 - cur[0]
            cur[0] = b2
            P.dve(lambda e: e.tensor_copy(out=Sf[:, c, :], in_=stf[:, a, :]),
                  r=[("B", "st", a)], w=[("B", "Sf", c)])
            P.pe(lambda e: e.matmul(C.psb[7][:, 0:256], lhsT=kef[:, lb, :], rhs=vv[:, c, :], start=True, stop=True),
                 r=[("A", "kef", lb), ("A", "v", c)], w=[("ps", 7)])
            P.dve(lambda e: e.scalar_tensor_tensor(out=stf[:, b2, :], in0=stf[:, a, :], scalar=dec[:, c, 0:1],
                                                   in1=C.psb[7][:, 0:256], op0=ALU.mult, op1=ALU.add),
                  r=[("B", "st", a), ("A", "dec", c, 0), ("ps", 7)], w=[("B", "st", b2)])

        def tile_end(t):
            tmps = [(C.tmpf[:, 0, :], ("tmpf", 0)), (C.tmpf[:, 1, :], ("tmpf", 1)), (C.tmpf[:, 2, :], ("tmpf", 2)),
                    (C.rstd[:], ("rstd",))]
            jobs = [(4, NS, 0, qf, "qf"), (4, -NS, 1, kf, "kf"), (5, NS, 0, qb, "qb"), (5, -NS, 1, kb, "kb")]
            for (bb, sc, src, dst, nm), (tb, tk) in zip(jobs, tmps):
                P.act(lambda e, bb=bb, sc=sc, tb=tb: e.activation(out=tb, in_=C.psb[bb][:], func=AF.Exp, scale=sc),
                      r=[("ps", bb)], w=[tk])
            for (bb, sc, src, dst, nm), (tb, tk) in zip(jobs, tmps):
                if src == 0:
                    P.dve(lambda e, dst=dst, tb=tb: e.scalar_tensor_tensor(
                        out=dst[:, tsl(t)], in0=C.psb[0][:], scalar=dk_scale, in1=tb, op0=ALU.mult, op1=ALU.mult),
                        r=[("ps", 0), tk], w=[("A", nm, t)])
                else:
                    P.dve(lambda e, dst=dst, tb=tb: e.tensor_tensor(out=dst[:, tsl(t)], in0=C.psb[1][:], in1=tb,
                                                                    op=ALU.mult),
                          r=[("ps", 1), tk], w=[("A", nm, t)])

        proj(0)
        for step in range(16 + 3):
            if 0 <= step - 2 < 16:
                stB(step - 2)
                if (step - 2) % 4 == 3:
                    tile_end((step - 2) // 4)
            if 0 <= step - 3 < 16:
                stC(step - 3)
            if step < 16:
                stA(step)
            if step >= 7 and (step - 7) % 4 == 0 and (step - 7) // 4 + 1 < NT:
                proj((step - 7) // 4 + 1)

        order = list(reversed(range(16)))

        def kvb(i):
            c = order[i]
            bk = 6 + i % 2
            P.pe(lambda e: e.matmul(C.psb[bk][:, 0:256], lhsT=keb[:, c, :], rhs=vv[:, c, :], start=True, stop=True),
                 r=[("A", "keb", c), ("A", "v", c)], w=[("ps", bk)])
        kvb(0)
        P.dve(lambda e: e.memset(stf[:, 0, :], 0.0), w=[("B", "st", 0)])
        cur[0] = 0
        for i, c in enumerate(order):
            bk = 6 + i % 2
            a, b2 = cur[0], 1 - cur[0]
            cur[0] = b2
            P.act(lambda e, c=c, a=a: e.activation(out=Sb[:, c, :], in_=stf[:, a, :], func=AF.Copy),
                  r=[("B", "st", a)], w=[("B", "Sb", c)])
            if i + 1 < 16:
                kvb(i + 1)
            P.dve(lambda e, c=c, bk=bk, a=a, b2=b2: e.scalar_tensor_tensor(
                out=stf[:, b2, :], in0=stf[:, a, :], scalar=dec[:, c, 1:2], in1=C.psb[bk][:, 0:256],
                op0=ALU.mult, op1=ALU.add),
                r=[("B", "st", a), ("A", "dec", c, 1), ("ps", bk)], w=[("B", "st", b2)])

        def chunks(t, extras=()):
            extras = list(extras)
            pob = (1, 2) if t % 2 == 0 else (5, 6)

            def scores(c):
                sb_ = 0 if c % 2 == 0 else 7
                cs = slice(c * 128, (c + 1) * 128)
                P.pe(lambda e: e.matmul(C.psb[sb_][:, 0:128], lhsT=kf[:, cs], rhs=qf[:, cs], start=True, stop=True),
                     r=[("A", "kf", t), ("A", "qf", t)], w=[("ps", sb_)])
                P.pe(lambda e: e.matmul(C.psb[sb_][:, 128:256], lhsT=kb[:, cs], rhs=qb[:, cs], start=True, stop=True),
                     r=[("A", "kb", t), ("A", "qb", t)], w=[("ps", sb_)])
            scores(4 * t)
            for c4 in range(4):
                c = 4 * t + c4
                mb = c % 2
                sb_ = 0 if c % 2 == 0 else 7
                cs = slice(c * 128, (c + 1) * 128)
                if c4 + 1 < 4:
                    scores(c + 1)
                P.dve(lambda e, mb=mb, sb_=sb_: e.tensor_tensor(out=ms[:, mb, :], in0=C.psb[sb_][:, 0:256],
                                                                in1=C.tri_bf[:, 0:256], op=ALU.mult),
                      r=[("ps", sb_), "tri_bf"], w=[("A", "ms", mb)])
                for vc in range(2):
                    ob = pob[vc]
                    osl = slice(c4 * 128, (c4 + 1) * 128)
                    vs = slice(vc * 128, (vc + 1) * 128)
                    P.pe(lambda e, ob=ob, osl=osl, vs=vs, c=c, mb=mb: e.matmul(
                        C.psb[ob][:, osl], lhsT=vv[:, c, vs], rhs=ms[:, mb, 0:128], start=True, stop=False),
                        r=[("A", "v", c), ("A", "ms", mb)], w=[("ps", ob)])
                    P.pe(lambda e, ob=ob, osl=osl, vs=vs, c=c, mb=mb: e.matmul(
                        C.psb[ob][:, osl], lhsT=vv[:, c, vs], rhs=ms[:, mb, 128:256], start=False, stop=False),
                        r=[("A", "v", c), ("A", "ms", mb)], w=[("ps", ob)])
                    P.pe(lambda e, ob=ob, osl=osl, vs=vs, c=c, cs=cs: e.matmul(
                        C.psb[ob][:, osl], lhsT=Sf[:, c, vs], rhs=qf[:, cs], start=False, stop=False),
                        r=[("B", "Sf", c), ("A", "qf", t)], w=[("ps", ob)])
                    P.pe(lambda e, ob=ob, osl=osl, vs=vs, c=c, cs=cs: e.matmul(
                        C.psb[ob][:, osl], lhsT=Sb[:, c, vs], rhs=qb[:, cs], start=False, stop=True),
                        r=[("B", "Sb", c), ("A", "qb", t)], w=[("ps", ob)])
                for _ in range(2):
                    if extras:
                        extras.pop(0)()
            while extras:
                extras.pop(0)()

        def R_pieces(t, w2=w2, w2k=w2k):
            out = []
            for vc in range(2):
                b = 3 + vc
                for k0 in range(0, KC, 2):
                    def mm(k0=k0, vc=vc, b=b):
                        for k in (k0, k0 + 1):
                            P.pe(lambda e, k=k: e.matmul(
                                C.psb[b][:], lhsT=w2[:, k, vc * 128:(vc + 1) * 128], rhs=C.H[:, t, k, :],
                                start=(k == 0), stop=(k == KC - 1)), r=[w2k, ("H", t, k)], w=[("ps", b)])
                    out.append(mm)
                out.append(lambda vc=vc, b=b: P.act(
                    lambda e: e.activation(out=C.tmpf[:, vc, :], in_=C.psb[b][:], func=AF.Silu),
                    r=[("ps", b)], w=[("tmpf", vc)]))
            return out

        otmp2 = B[:, 10240:12288].bitcast(F32).rearrange("p (v n) -> p v n", v=2)
        o2keys = ([("B", "otmp")], [("B", "e1"), ("B", "ee")])

        def N_a(t):
            pob = (1, 2) if t % 2 == 0 else (5, 6)
            for vc in range(2):
                P.act(lambda e, vc=vc: e.activation(out=C.sq[:, vc, :], in_=C.psb[pob[vc]][:], func=AF.Square),
                      r=[("ps", pob[vc])], w=[("sq", vc)])
                P.pe(lambda e, vc=vc: e.matmul(C.psb[3][:], lhsT=C.ones_bf[:], rhs=C.sq[:, vc, :],
                                               start=(vc == 0), stop=(vc == 1)),
                     r=[("sq", vc), "ones_bf"], w=[("ps", 3)])
            for vc in range(2):
                P.dve(lambda e, vc=vc: e.tensor_tensor(out=otmp2[:, vc, :], in0=C.psb[pob[vc]][:], in1=C.tmpf[:, vc, :],
                                                       op=ALU.mult),
                      r=[("ps", pob[vc]), ("tmpf", vc)], w=o2keys[vc])

        def N_b(t):
            P.act(lambda e: e.activation(out=C.tmpf[:, 2, :], in_=C.psb[3][:], func=AF.Ln, bias=EPS, scale=1.0 / 256),
                  r=[("ps", 3)], w=[("tmpf", 2)])
            P.act(lambda e: e.activation(out=C.rstd[:], in_=C.tmpf[:, 2, :], func=AF.Exp, scale=-0.5),
                  r=[("tmpf", 2)], w=["rstd"])
            for vc in range(2):
                gc = VC_HN + j * 2 + vc
                P.dve(lambda e, vc=vc, gc=gc: e.scalar_tensor_tensor(
                    out=on[:, vc, :], in0=otmp2[:, vc, :], scalar=C.vecs[:, gc:gc + 1], in1=C.rstd[:],
                    op0=ALU.mult, op1=ALU.mult), r=o2keys[vc] + ["vecs", "rstd"], w=[("A", "on", vc)])

        def W_pieces(t, w3=w3, w3k=w3k):
            def piece(i):
                b = 0 if i % 2 == 0 else 7
                for vc in range(2):
                    P.pe(lambda e, vc=vc: e.matmul(
                        C.psb[b][:], lhsT=w3[:, vc, i * 128:(i + 1) * 128], rhs=on[:, vc, :],
                        start=(vc == 0), stop=(vc == 1)), r=[w3k, ("A", "on", vc)], w=[("ps", b)])
                P.dve(lambda e: e.tensor_tensor(out=C.X[:, i, tsl(t)], in0=C.psb[b][:], in1=C.X[:, i, tsl(t)], op=ALU.add),
                      r=[("ps", b), ("X", i, t)], w=[("X", i, t)])
            return [lambda i=i: piece(i) for i in range(KC)]

        for pc in R_pieces(0):
            pc()
        chunks(0)
        chunks(1)
        for t in range(NT):
            N_a(t)
            if t + 2 < NT:
                chunks(t + 2)
            N_b(t)
            wp = W_pieces(t)
            rp = R_pieces(t + 1) if t + 1 < NT else []
            for _ in range(min(5, len(rp))):
                rp.pop(0)()
            while wp or rp:
                if wp:
                    wp.pop(0)()
                if rp:
                    rp.pop(0)()
            if h == 3 and hook is not None:
                hook(t, bank=3)


FULL_SPEC = []
for _li in range(DEPTH):
    FULL_SPEC.append(("ffn", _li, 0))
    FULL_SPEC.append(("gla", _li) if _li % 2 == 0 else ("mla", _li))
    FULL_SPEC.append(("ffn", _li, 1))
FULL_SPEC.append(("final",))


def kernel(**inputs):
    per_core = _host_prepare(inputs)
    nc = build_program(FULL_SPEC)
    res = run_bass_kernel_spmd(nc, per_core, core_ids=list(range(len(per_core))))
    out = np.stack([np.ascontiguousarray(r["outT"].T) for r in res.results], axis=0)
    return out.astype(np.float32)
```

```python
from collections import defaultdict
from contextlib import ExitStack
import math
import os
import numpy as np
import concourse.bass as bass
import concourse.mybir as mybir
from concourse.bass_utils import run_bass_kernel_spmd

F32 = mybir.dt.float32
BF16 = mybir.dt.bfloat16
I32 = mybir.dt.int32
ALU = mybir.AluOpType
AF = mybir.ActivationFunctionType

ENGS = ("pe", "act", "dve", "pool", "sp")

D = 1024
S = 2048
DEPTH = 4
DFF = 2816
NT = 4
TT = 512
KC = 8
EPS = 1e-6


class Op:
    __slots__ = ("eng", "fn", "waits", "signal", "idx", "dma_sem", "dma_val", "ordinal")

    def __init__(self, eng, fn, idx):
        self.eng = eng
        self.fn = fn
        self.idx = idx
        self.waits = {}
        self.signal = False
        self.dma_sem = None
        self.dma_val = 0
        self.ordinal = 0


class Prog:
    def __init__(self, nc, stack):
        self.nc = nc
        self.stack = stack
        self.ops = {e: [] for e in ENGS}
        self.lastw = {}
        self.readers = {}
        self.bufkeys = defaultdict(set)
        self.seen = {e: {} for e in ENGS}
        self.dma_cum = {}
        self.raw_window = 2
        self.full_same_engine_sync = os.environ.get("K_FULLSYNC", "1") == "1"

    def sb(self, name, shape, dt):
        return self.stack.enter_context(self.nc.sbuf_tensor("sb_" + name, list(shape), dt))

    def ps(self, name, shape, dt=F32):
        return self.stack.enter_context(self.nc.psum_tensor("pp_" + name, list(shape), dt))

    def _conf(self, key):
        n = len(key)
        for k2 in self.bufkeys.get(key[0], ()):
            m = len(k2)
            if m <= n:
                if key[:m] == k2:
                    yield k2
            elif k2[:n] == key:
                yield k2

    def _need(self, op, d, kind):
        if d is op:
            return
        if d.dma_sem is not None:
            src = ("dma", d.dma_sem)
            val = d.dma_val
        else:
            src = d.eng
            val = d.idx
            if d.eng == op.eng:
                if op.eng == "pe" or op.eng == "sp":
                    return
                if not self.full_same_engine_sync:
                    if kind != "raw":
                        return
                    if op.idx - d.idx > self.raw_window:
                        return
        if self.seen[op.eng].get(src, -1) >= val:
            return
        self.seen[op.eng][src] = val
        op.waits[src] = val
        d.signal = True

    def add(self, eng, fn, r=(), w=(), dma_sem=None):
        op = Op(eng, fn, len(self.ops[eng]))
        if dma_sem is not None:
            op.dma_sem = dma_sem
            self.dma_cum[dma_sem] = self.dma_cum.get(dma_sem, 0) + 16
            op.dma_val = self.dma_cum[dma_sem]
        r = [tuple(k) if isinstance(k, (tuple, list)) else (k,) for k in r]
        w = [tuple(k) if isinstance(k, (tuple, list)) else (k,) for k in w]
        for k in r:
            for k2 in self._conf(k):
                d = self.lastw.get(k2)
                if d is not None:
                    self._need(op, d, "raw")
                if k[0] == "ps":
                    for d in self.readers.get(k2, {}).values():
                        if d.eng != eng:
                            self._need(op, d, "rar")
        for k in w:
            for k2 in list(self._conf(k)):
                d = self.lastw.get(k2)
                if d is not None:
                    self._need(op, d, "waw")
                for d in self.readers.get(k2, {}).values():
                    self._need(op, d, "war")
        rid = eng if dma_sem is None else ("dma", dma_sem)
        for k in r:
            self.bufkeys[k[0]].add(k)
            self.readers.setdefault(k, {})[rid] = op
        for k in w:
            n = len(k)
            for k2 in list(self._conf(k)):
                if len(k2) >= n:
                    self.lastw[k2] = op
                    self.readers[k2] = {}
            self.bufkeys[k[0]].add(k)
            self.lastw[k] = op
            self.readers[k] = {}
        self.ops[eng].append(op)
        return op

    def pe(self, fn, r=(), w=()):
        return self.add("pe", fn, r, w)

    def act(self, fn, r=(), w=()):
        return self.add("act", fn, r, w)

    def dve(self, fn, r=(), w=()):
        return self.add("dve", fn, r, w)

    def pool(self, fn, r=(), w=()):
        return self.add("pool", fn, r, w)

    def dma(self, q, out, in_, r=(), w=(), sem=None, **kw):
        return self.add(q, lambda e: e.dma_start(out=out, in_=in_, **kw), r, w, dma_sem=sem)

    def fence(self, region):
        return self.add("sp", lambda e: e.nop(), r=(), w=[(region,)])

    def final_wait(self, eng, keys):
        return self.add(eng, None, r=keys)

    def emit(self):
        nc = self.nc
        st = self.stack
        EPOCH = 3000
        ordmap = {}
        nsem = {}
        for e in ENGS:
            c = 0
            m = {}
            for op in self.ops[e]:
                if op.signal and op.dma_sem is None:
                    c += 1
                    m[op.idx] = c
            ordmap[e] = m
            nsem[e] = max(1, (c + EPOCH - 1) // EPOCH)
        esem = {e: [st.enter_context(nc.semaphore("s_%s%d" % (e, i))) for i in range(nsem[e])] for e in ENGS}
        dsem = {n: st.enter_context(nc.semaphore("d_%d" % i)) for i, n in enumerate(self.dma_cum)}

        def run(e, name):
            for op in self.ops[name]:
                for src, v in op.waits.items():
                    if isinstance(src, tuple):
                        e.wait_ge(dsem[src[1]], v)
                    else:
                        o = ordmap[src][v] - 1
                        e.wait_ge(esem[src][o // EPOCH], o % EPOCH + 1)
                if op.fn is None:
                    continue
                ins = op.fn(e)
                if op.dma_sem is not None:
                    ins.then_inc(dsem[op.dma_sem], 16)
                elif op.signal:
                    o = ordmap[name][op.idx] - 1
                    ins.then_inc(esem[name][o // EPOCH], 1)

        with nc.Block() as block:
            @block.tensor
            def _(e):
                run(e, "pe")

            @block.scalar
            def _(e):
                run(e, "act")

            @block.vector
            def _(e):
                run(e, "dve")

            @block.gpsimd
            def _(e):
                run(e, "pool")

            @block.sync
            def _(e):
                run(e, "sp")


VC_FFN = 0
VC_MIX = 64
VC_FIN = 96
VC_QN = 104
VC_KVN = 116
VC_HN = 120
NVEC = 124
CC_INVF = 0
CC_SH = 1
CC_TRI = 4
CC_MASK = CC_TRI
CC_ONES = 4 + 512
NCONST = 4 + 512 + 128

GLA_HC = 768
GLA_WIN_COLS = 4 * GLA_HC + 32


def _chunk_cols(v):
    v = np.asarray(v, np.float32)
    return np.ascontiguousarray(v.reshape(-1, 128).T)


def _host_consts():
    c = np.zeros((128, NCONST), np.float32)
    inv = (1.0 / (10000.0 ** (np.arange(0, 64, 2, dtype=np.float32) / np.float32(64)))).astype(np.float32)
    p = np.arange(128)
    c[:, CC_INVF] = inv[p % 32]
    c[:, CC_SH] = np.where(p < 64, 1.5 * math.pi, np.where((p % 64) < 32, 0.0, math.pi))
    s = np.arange(128)[:, None]
    t = np.arange(128)[None, :]
    c[:, CC_TRI:CC_TRI + 128] = (s <= t)
    c[:, CC_TRI + 128:CC_TRI + 256] = (s > t)
    c[:, CC_TRI + 256:CC_TRI + 384] = (s >= t)
    c[:, CC_TRI + 384:CC_TRI + 512] = (s < t)
    c[:, CC_ONES:CC_ONES + 128] = 1.0
    return c


def _host_prepare(inp):
    f = lambda a: np.ascontiguousarray(np.asarray(a, np.float32))
    vec = np.zeros((128, NVEC), np.float32)
    fn = np.asarray(inp["ffn_norm"], np.float32)
    for li in range(DEPTH):
        for w in range(2):
            vec[:, VC_FFN + (li * 2 + w) * 8: VC_FFN + (li * 2 + w) * 8 + 8] = _chunk_cols(fn[li, w])
        vec[:, VC_MIX + li * 8: VC_MIX + li * 8 + 8] = _chunk_cols(np.asarray(inp["mix_norm"])[li])
    vec[:, VC_FIN:VC_FIN + 8] = _chunk_cols(inp["final_norm"])
    for j in range(2):
        vec[:, VC_QN + j * 6: VC_QN + j * 6 + 6] = _chunk_cols(np.asarray(inp["mla_q_norm"])[j])
        vec[:, VC_KVN + j * 2: VC_KVN + j * 2 + 2] = _chunk_cols(np.asarray(inp["mla_kv_norm"])[j])
        vec[:, VC_HN + j * 2: VC_HN + j * 2 + 2] = _chunk_cols(np.asarray(inp["gla_head_norm"])[j])
    gw = np.asarray(inp["gla_w_in"], np.float32)
    cols = []
    for h in range(4):
        cols += list(range(h * 128, (h + 1) * 128))
        cols += list(range(512 + h * 128, 512 + (h + 1) * 128))
        cols += list(range(1024 + h * 256, 1024 + (h + 1) * 256))
        cols += list(range(2048 + h * 256, 2048 + (h + 1) * 256))
    cols += list(range(3072, 3104))
    gla_win = np.ascontiguousarray(gw[:, :, cols])
    g2 = np.concatenate([np.asarray(inp["gla_w_gate2"], np.float32),
                         np.asarray(inp["gla_b_gate"], np.float32)[:, :, None, :]], axis=2)
    g2aug = np.ascontiguousarray(g2.transpose(0, 2, 1, 3).reshape(2, 17, 1024))
    mw = np.asarray(inp["mla_w_in"], np.float32)
    sw = list(range(1024 + 32, 1024 + 64)) + list(range(1024, 1024 + 32))
    mla_win = np.ascontiguousarray(np.concatenate([mw, mw[:, :, [c + 0 for c in sw]]], axis=2))
    uq = np.asarray(inp["mla_w_uq"], np.float32)
    cols = []
    for h in range(8):
        b = h * 192
        cols += list(range(b, b + 192))
        cols += list(range(b + 160, b + 192)) + list(range(b + 128, b + 160))
    mla_wuq = np.ascontiguousarray(uq[:, :, cols])
    shared = {
        "vecs": vec, "consts": _host_consts(),
        "w_gu": f(inp["ffn_w_gu"]).reshape(8, D, 2 * DFF), "w_dn": f(inp["ffn_w_down"]).reshape(8, DFF, D),
        "gla_win": gla_win, "g2aug": g2aug, "gla_wout": f(inp["gla_w_out"]),
        "mla_win": mla_win, "mla_wuq": mla_wuq, "mla_wukv": f(inp["mla_w_ukv"]), "mla_wout": f(inp["mla_w_out"]),
    }
    x = np.asarray(inp["x"], np.float32)
    pos = np.asarray(inp["positions"], np.int32)
    per_core = []
    for b in range(x.shape[0]):
        m = dict(shared)
        m["xT"] = np.ascontiguousarray(x[b].T)
        m["pos"] = np.ascontiguousarray(np.broadcast_to(pos[b][None, :], (128, S)))
        per_core.append(m)
    return per_core


class Ctx:
    pass


def tsl(t):
    return slice(t * TT, (t + 1) * TT)


def build_program(spec, dbg=None):
    nc = bass.Bass("TRN2", target_bir_lowering=False)
    dr = lambda name, shape, dt=F32, kind="ExternalInput": nc.dram_tensor(name, list(shape), dt, kind=kind).ap()
    C = Ctx()
    C.xT_d = dr("xT", [D, S])
    C.pos_d = dr("pos", [128, S], I32)
    C.vecs_d = dr("vecs", [128, NVEC])
    C.consts_d = dr("consts", [128, NCONST])
    C.w_gu = dr("w_gu", [8, D, 2 * DFF])
    C.w_dn = dr("w_dn", [8, DFF, D])
    C.gla_win = dr("gla_win", [2, D, GLA_WIN_COLS])
    C.g2aug = dr("g2aug", [2, 17, 1024])
    C.gla_wout = dr("gla_wout", [2, D, D])
    C.mla_win = dr("mla_win", [2, D, 1152])
    C.mla_wuq = dr("mla_wuq", [2, 768, 2048])
    C.mla_wukv = dr("mla_wukv", [2, 256, 2048])
    C.mla_wout = dr("mla_wout", [2, D, D])
    C.out_d = dr("outT", [D, S], F32, kind="ExternalOutput")
    C.dbg_d = {}
    if dbg:
        for name, shape in dbg.items():
            C.dbg_d[name] = dr("dbg_" + name, shape, F32, kind="ExternalOutput")

    with ExitStack() as st:
        P = Prog(nc, st)
        C.P = P
        C.X = P.sb("X", [128, KC, S], F32)
        C.H = P.sb("H", [128, NT, KC, TT], BF16)
        C.A = P.sb("A", [128, 16384], BF16)
        C.B = P.sb("B", [128, 12288], BF16)
        C.W = [P.sb("W%d" % i, [128, 4096], BF16) for i in range(4)]
        C.R = P.sb("R", [128, S], F32)
        C.vecs = P.sb("vecs", [128, NVEC], F32)
        C.consts = P.sb("consts", [128, 4 + 128], F32)
        C.tri_bf = P.sb("tri_bf", [128, 512], BF16)
        C.ones_bf = P.sb("ones_bf", [128, 128], BF16)
        C.sq = P.sb("sq", [128, 2, TT], BF16)
        C.tmpf = P.sb("tmpf", [128, 3, TT], F32)
        C.rstd = P.sb("rstd", [128, TT], F32)
        C.g2 = P.sb("g2", [128, 1024], BF16)
        C.psb = [P.ps("ps%d" % i, [128, TT]) for i in range(8)]
        C.wslot = 0
        C.psi = 0

        P.dma("sp", C.vecs[:], C.vecs_d, w=["vecs"], sem="vecs")
        P.dma("sp", C.consts[:, 0:4], C.consts_d[:, 0:4], w=[("consts", 0)], sem="consts")
        P.dma("sp", C.consts[:, 4:132], C.consts_d[:, CC_ONES:CC_ONES + 128], w=[("consts", 1)], sem="consts1")
        P.dma("pool", C.tri_bf[:], C.consts_d[:, CC_TRI:CC_TRI + 512], w=["tri_bf"], sem="tri")
        xv = C.xT_d.rearrange("(k p) n -> p k n", p=128)
        for t in range(NT):
            P.dma("sp", C.X[:, :, tsl(t)], xv[:, :, tsl(t)], w=[("X", k, t) for k in range(KC)], sem="x%d" % t)
        P.dve(lambda e: e.memset(C.ones_bf[:], 1.0), w=["ones_bf"])

        C.rope_dirty = True
        C.side = []

        def gcol_of(stg):
            if stg[0] == "ffn":
                return VC_FFN + (stg[1] * 2 + stg[2]) * 8
            if stg[0] in ("mla", "gla"):
                return VC_MIX + stg[1] * 8
            return None

        prenormed = False
        for sidx, stg in enumerate(spec):
            nxt = spec[sidx + 1] if sidx + 1 < len(spec) else None
            g_next = gcol_of(nxt) if nxt is not None else None
            if os.environ.get("K_NOHOOK") == "1":
                g_next = None
            hook = (lambda t, bank=None, g=g_next: norm_tile(C, g, t, bank=bank)) if g_next is not None else None
            if stg[0] == "ffn" and nxt is not None and os.environ.get("K_NOHOIST") != "1":
                if nxt[0] == "mla" and C.rope_dirty:
                    P.fence("B")
                    C.side += rope_table_pieces(C)
                elif nxt[0] == "gla":
                    C.side += gla_setup_pieces(C, nxt[1])
            if stg[0] == "ffn":
                emit_ffn(C, stg[1], stg[2], prenormed, hook)
            elif stg[0] == "mla":
                emit_mla(C, stg[1], prenormed, hook)
            elif stg[0] == "gla":
                emit_gla(C, stg[1], prenormed, hook)
            elif stg[0] == "final":
                emit_final(C, True)
            elif stg[0] == "rawout":
                emit_final(C, False)
            prenormed = hook is not None
        P.emit()
    return nc


def psum(C, n=8, base=0):
    b = base + (C.psi % n)
    C.psi += 1
    return b


def wslot(C):
    s = C.wslot % 4
    C.wslot += 1
    return s


def rope_table_pieces(C):
    P = C.P
    C.rope_dirty = False
    two_pi = 2.0 * math.pi
    b0i = C.B[:, 0:2 * S].bitcast(I32)
    b0f = C.B[:, 0:2 * S].bitcast(F32)
    b1f = C.B[:, 2 * S:4 * S].bitcast(F32)
    R = C.R
    out = [lambda: P.dma("sp", b0i, C.pos_d, w=[("B", "b0")], sem="pos")]
    for c in range(NT):
        cs = tsl(c)
        k0, k1, kr = ("B", "b0", c), ("B", "b1", c), ("R", "tab", c)
        out += [
            lambda cs=cs, k0=k0, k1=k1: P.dve(lambda e: e.tensor_copy(out=b1f[:, cs], in_=b0i[:, cs]), r=[k0], w=[k1]),
            lambda cs=cs, k1=k1, kr=kr: P.dve(lambda e: e.tensor_scalar(
                out=R[:, cs], in0=b1f[:, cs], scalar1=C.consts[:, CC_INVF:CC_INVF + 1],
                scalar2=C.consts[:, CC_SH:CC_SH + 1], op0=ALU.mult, op1=ALU.add), r=[k1, "consts"], w=[kr]),
            lambda cs=cs, k0=k0, kr=kr: P.dve(lambda e: e.tensor_scalar(
                out=b0i[:, cs], in0=R[:, cs], scalar1=1.0 / two_pi, scalar2=None, op0=ALU.mult), r=[kr], w=[k0]),
            lambda cs=cs, k0=k0, k1=k1: P.dve(lambda e: e.tensor_copy(out=b1f[:, cs], in_=b0i[:, cs]), r=[k0], w=[k1]),
            lambda cs=cs, k1=k1, kr=kr: P.dve(lambda e: e.scalar_tensor_tensor(
                out=R[:, cs], in0=b1f[:, cs], scalar=-two_pi, in1=R[:, cs], op0=ALU.mult, op1=ALU.add),
                r=[k1, kr], w=[kr]),
            lambda cs=cs, k0=k0, kr=kr: P.dve(lambda e: e.tensor_scalar(
                out=b0f[:, cs], in0=R[:, cs], scalar1=0.0, scalar2=two_pi, op0=ALU.is_lt, op1=ALU.mult),
                r=[kr], w=[k0]),
            lambda cs=cs, k0=k0, kr=kr: P.dve(lambda e: e.tensor_tensor(
                out=R[:, cs], in0=R[:, cs], in1=b0f[:, cs], op=ALU.add), r=[kr, k0], w=[kr]),
            lambda cs=cs, kr=kr: P.dve(lambda e: e.tensor_scalar(
                out=R[:, cs], in0=R[:, cs], scalar1=-math.pi, scalar2=None, op0=ALU.add), r=[kr], w=[kr]),
            lambda cs=cs, kr=kr: P.act(lambda e: e.activation(out=R[:, cs], in_=R[:, cs], func=AF.Sin), r=[kr], w=[kr]),
        ]
    return out


def emit_rope_tables(C):
    for pc in rope_table_pieces(C):
        pc()


def gla_setup_pieces(C, li):
    P = C.P
    j = li // 2
    P.fence("R")
    C.rope_dirty = True
    C.gla_ready = li
    gflat = C.R[:, :].bitcast(BF16)
    out = [lambda: P.dve(lambda e: e.memset(C.g2[:], 0.0), w=["g2"]),
           lambda: P.dma("pool", C.g2[0:17, :], C.g2aug[j], w=["g2"], sem="g2")]
    for c in range(4):
        cs = slice(c * 1024, (c + 1) * 1024)
        out.append(lambda cs=cs, c=c: P.dve(lambda e: e.memset(gflat[:, cs], 0.0), w=[("R", "ms", c)]))
        out.append(lambda cs=cs, c=c: P.dve(lambda e: e.memset(gflat[0:32, cs], 1.0), w=[("R", "ms", c)]))
    return out


def emit_gla_setup(C, li):
    for pc in gla_setup_pieces(C, li):
        pc()


def emit_rmsnorm(C, n_chunks, src_fn, src_keys, gcol, dst_fn, dst_keys, t, inv_n, stats_ps=None, bank=None):
    P = C.P
    if stats_ps is None:
        b = psum(C) if bank is None else bank
        for k in range(n_chunks):
            q = k % 2
            P.act(lambda e, k=k, q=q: e.activation(out=C.sq[:, q, :], in_=src_fn(k), func=AF.Square),
                  r=[src_keys(k)], w=[("sq", q)])
            P.pe(lambda e, k=k, q=q, b=b: e.matmul(C.psb[b][:], lhsT=C.ones_bf[:], rhs=C.sq[:, q, :],
                                                   start=(k == 0), stop=(k == n_chunks - 1)),
                 r=[("sq", q), "ones_bf"], w=[("ps", b)])
    else:
        b = stats_ps
    P.act(lambda e, b=b: e.activation(out=C.tmpf[:, 2, :], in_=C.psb[b][:], func=AF.Ln, bias=EPS, scale=inv_n),
          r=[("ps", b)], w=[("tmpf", 2)])
    P.act(lambda e: e.activation(out=C.rstd[:], in_=C.tmpf[:, 2, :], func=AF.Exp, scale=-0.5), r=[("tmpf", 2)], w=["rstd"])
    for k in range(n_chunks):
        P.dve(lambda e, k=k: e.scalar_tensor_tensor(
            out=dst_fn(k), in0=src_fn(k), scalar=C.vecs[:, gcol + k:gcol + k + 1], in1=C.rstd[:],
            op0=ALU.mult, op1=ALU.mult),
            r=[src_keys(k), "rstd", "vecs"], w=[dst_keys(k)])


def norm_tile(C, gcol, t, bank=None):
    emit_rmsnorm(C, KC, lambda k: C.X[:, k, tsl(t)], lambda k: ("X", k, t), gcol,
                 lambda k: C.H[:, t, k, :], lambda k: ("H", t, k), t, 1.0 / D, bank=bank)


def emit_norm_to_H(C, gcol):
    for t in range(NT):
        norm_tile(C, gcol, t)


def emit_final(C, do_norm):
    P = C.P
    ov = C.out_d.rearrange("(k p) n -> p k n", p=128)
    if not do_norm:
        for t in range(NT):
            P.dma("sp", ov[:, :, tsl(t)], C.X[:, :, tsl(t)], r=[("X", k, t) for k in range(KC)], w=[("out", t)],
                  sem="o%d" % t)
    else:
        P.fence("A")
        P.fence("B")
        for t in range(NT):
            reg, nm = (C.A, "A") if t % 2 == 0 else (C.B, "B")
            of = reg[:, 0:2 * KC * TT].bitcast(F32).rearrange("p (k n) -> p k n", k=KC)
            emit_rmsnorm(C, KC, lambda k, t=t: C.X[:, k, tsl(t)], lambda k, t=t: ("X", k, t), VC_FIN,
                         lambda k, of=of: of[:, k, :], lambda k, nm=nm: (nm, "o", k), t, 1.0 / D)
            P.dma("sp", ov[:, :, tsl(t)], of, r=[(nm, "o")], w=[("out", t)], sem="o%d" % t)
    P.final_wait("sp", [("out", t) for t in range(NT)])


def load_w(C, src_aps, shape_view):
    P = C.P
    s = wslot(C)
    n = 1
    for d in shape_view:
        n *= d
    assert n <= 4096
    flat = C.W[s][:, 0:n]
    if len(shape_view) == 2:
        v = flat.rearrange("p (a b) -> p a b", a=shape_view[0])
    elif len(shape_view) == 3:
        v = flat.rearrange("p (a b c) -> p a b c", a=shape_view[0], b=shape_view[1])
    else:
        v = flat
    if isinstance(src_aps, (list, tuple)):
        keys = []
        for g, src in enumerate(src_aps):
            P.dma("pool", v[:, :, g, :], src, w=[("W", s, g)], sem="w%d_%d" % (s, g))
            keys.append(("W", s, g))
        return v, keys
    P.dma("pool", v, src_aps, w=[("W", s)], sem="w%d_0" % s)
    return v, ("W", s)


def emit_ffn(C, li, which, prenormed=False, hook=None):
    P = C.P
    fi = li * 2 + which
    if not prenormed:
        emit_norm_to_H(C, VC_FFN + fi * 8)
    P.fence("A")
    aT = C.A[:, 0:8 * S].rearrange("p (j n) -> p j n", j=8)
    wgu = C.w_gu[fi].rearrange("(k p) (g n) -> p k g n", p=128, g=2)
    wdn = C.w_dn[fi].rearrange("(j p) n -> p j n", p=128)
    pieces = [(0, 4), (4, 8), (8, 11)]
    for (t0, t1) in pieces:
        nj = (t1 - t0) * 2
        for ti in range(t0, t1):
            wv, wks = load_w(C, [wgu[:, :, g, ti * 256:(ti + 1) * 256] for g in range(2)], (KC, 2, 256))
            for jj in range(2):
                j = (ti - t0) * 2 + jj
                for t in range(NT):
                    bg = psum(C)
                    bu = psum(C)
                    for g, b in ((0, bg), (1, bu)):
                        for k in range(KC):
                            P.pe(lambda e, k=k, g=g, b=b, jj=jj, t=t, wv=wv: e.matmul(
                                C.psb[b][:], lhsT=wv[:, k, g, jj * 128:(jj + 1) * 128], rhs=C.H[:, t, k, :],
                                start=(k == 0), stop=(k == KC - 1)),
                                r=[wks[g], ("H", t, k)], w=[("ps", b)])
                    q = (j * NT + t) % 2
                    P.act(lambda e, bg=bg, q=q: e.activation(out=C.tmpf[:, q, :], in_=C.psb[bg][:], func=AF.Silu),
                          r=[("ps", bg)], w=[("tmpf", q)])
                    P.dve(lambda e, bu=bu, q=q, j=j, t=t: e.tensor_tensor(
                        out=aT[:, j, tsl(t)], in0=C.psb[bu][:], in1=C.tmpf[:, q, :], op=ALU.mult),
                        r=[("ps", bu), ("tmpf", q)], w=[("A", "a", j, t)])
                    if C.side:
                        C.side.pop(0)()
        wds = []
        for c0 in range(0, nj, 4):
            n = min(4, nj - c0)
            r0 = t0 * 2 + c0
            wds.append(load_w(C, wdn[:, r0:r0 + n, :], (n, D)))
        last_piece = (t1 == pieces[-1][1])
        for t in range(NT):
            if last_piece and hook is not None and t >= 1:
                hook(t - 1)
            for i in range(KC):
                b = psum(C)
                for j in range(nj):
                    wv, wk = wds[j // 4]
                    P.pe(lambda e, b=b, j=j, i=i, t=t, wv=wv, nj=nj: e.matmul(
                        C.psb[b][:], lhsT=wv[:, j % 4, i * 128:(i + 1) * 128], rhs=aT[:, j, tsl(t)],
                        start=(j == 0), stop=(j == nj - 1)),
                        r=[wk, ("A", "a", j, t)], w=[("ps", b)])
                P.dve(lambda e, b=b, i=i, t=t: e.scalar_tensor_tensor(
                    out=C.X[:, i, tsl(t)], in0=C.psb[b][:], scalar=0.5, in1=C.X[:, i, tsl(t)],
                    op0=ALU.mult, op1=ALU.add),
                    r=[("ps", b), ("X", i, t)], w=[("X", i, t)])
    while C.side:
        C.side.pop(0)()
    if hook is not None:
        hook(NT - 1)


def emit_rope(C, b, dst, dst_key, t):
    P = C.P
    P.dve(lambda e: e.tensor_tensor(out=C.tmpf[0:64, 0, :], in0=C.psb[b][0:64, :], in1=C.R[0:64, tsl(t)], op=ALU.mult),
          r=[("ps", b), ("R",)], w=[("tmpf", 0)])
    P.dve(lambda e: e.tensor_tensor(out=C.tmpf[0:64, 1, :], in0=C.psb[b][64:128, :], in1=C.R[64:128, tsl(t)], op=ALU.mult),
          r=[("ps", b), ("R",)], w=[("tmpf", 1)])
    P.dve(lambda e: e.tensor_tensor(out=dst, in0=C.tmpf[0:64, 0, :], in1=C.tmpf[0:64, 1, :], op=ALU.add),
          r=[("tmpf", 0), ("tmpf", 1)], w=[dst_key])


def emit_mla(C, li, prenormed=False, hook=None):
    P = C.P
    j = li // 2
    if not prenormed:
        emit_norm_to_H(C, VC_MIX + li * 8)
    P.fence("A")
    P.fence("B")
    if C.rope_dirty:
        emit_rope_tables(C)
        P.fence("B")
    A, B = C.A, C.B
    attn = A[:, :].rearrange("p (h n) -> p h n", h=8)
    ctmps = [A[:, 8192 * i:8192 * (i + 1)].bitcast(F32).rearrange("p (m n) -> p m n", m=8) for i in range(2)]
    qn = B[:, 0:2048]
    qpe = B[:, 2048:4096]
    kn = B[:, 4096:6144]
    vv = B[:, 6144:8192].rearrange("p (c n) -> p c n", c=16)
    kpe = B[:, 8192:10240]
    PT = B[:, 10240:12288].rearrange("p (i n) -> p i n", i=4)
    acc = C.tmpf[:, 0, :]
    ones_f = C.consts[:, 4:132]

    P.dve(lambda e: e.memset(kpe[64:128, :], 0.0), w=[("B", "kpe_pad")])
    P.dve(lambda e: e.memset(qpe[64:128, :], 0.0), w=[("B", "qpe_pad")])

    win = C.mla_win[j].rearrange("(k p) n -> p k n", p=128)
    wt = [load_w(C, win[:, :, 0:512], (KC, 512)), load_w(C, win[:, :, 512:1024], (KC, 512)),
          load_w(C, win[:, :, 1024:1152], (KC, 128))]
    for t in range(NT):
        pend = None
        ctmp = ctmps[t % 2]
        cb = t % 2
        for m in range(9):
            b = psum(C, 4)
            wv, wk = wt[m // 4]
            c0 = (m % 4) * 128
            for k in range(KC):
                P.pe(lambda e, b=b, k=k, wv=wv, c0=c0, t=t: e.matmul(
                    C.psb[b][:], lhsT=wv[:, k, c0:c0 + 128], rhs=C.H[:, t, k, :], start=(k == 0), stop=(k == KC - 1)),
                    r=[wk, ("H", t, k)], w=[("ps", b)])
            if pend is not None:
                pend()
                pend = None
            if m < 8:
                q = m % 2
                P.act(lambda e, b=b, m=m, ctmp=ctmp: e.activation(out=ctmp[:, m, :], in_=C.psb[b][:], func=AF.Copy),
                      r=[("ps", b)], w=[("A", "ctmp", cb, m)])
                P.act(lambda e, b=b, q=q: e.activation(out=C.sq[:, q, :], in_=C.psb[b][:], func=AF.Square),
                      r=[("ps", b)], w=[("sq", q)])
                sb_ = (4 if m < 6 else 5) + 2 * (t % 2)

                def stat(m=m, q=q, sb_=sb_):
                    P.pe(lambda e: e.matmul(C.psb[sb_][:], lhsT=C.ones_bf[:], rhs=C.sq[:, q, :],
                                            start=(m == 0 or m == 6), stop=(m == 5 or m == 7)),
                         r=[("sq", q), "ones_bf"], w=[("ps", sb_)])
                pend = stat
            else:
                emit_rope(C, b, kpe[0:64, tsl(t)], ("B", "kpe", t), t)
        if pend is not None:
            pend()
        emit_rmsnorm(C, 6, lambda k, ctmp=ctmp: ctmp[:, k, :], lambda k, cb=cb: ("A", "ctmp", cb, k), VC_QN + j * 6,
                     lambda k, t=t: C.H[:, t, k, :], lambda k, t=t: ("H", t, k), t, 1.0 / 768, stats_ps=4 + 2 * (t % 2))
        emit_rmsnorm(C, 2, lambda k, ctmp=ctmp: ctmp[:, 6 + k, :], lambda k, cb=cb: ("A", "ctmp", cb, 6 + k), VC_KVN + j * 2,
                     lambda k, t=t: C.H[:, t, 6 + k, :], lambda k, t=t: ("H", t, 6 + k), t, 1.0 / 256, stats_ps=5 + 2 * (t % 2))
    if os.environ.get("MLA_STOP") == "a":
        return
    P.fence("A")
    wukv_d = C.mla_wukv[j].rearrange("(k p) n -> p k n", p=128)
    wuq_d = C.mla_wuq[j].rearrange("(k p) n -> p k n", p=128)
    wout_d = C.mla_wout[j].rearrange("(k p) n -> p k n", p=128)
    sm_scale = 192.0 ** -0.5
    for h in range(8):
        hh = h % 2
        if hh == 0:
            sl = wslot(C)
            wuq = C.W[sl][:, 0:3072].rearrange("p (k n) -> p k n", k=6)
            wukv = C.W[sl][:, 3072:4096].rearrange("p (k n) -> p k n", k=2)
            wuqk, wukvk = ("W", sl, 0), ("W", sl, 1)
            P.dma("pool", wuq, wuq_d[:, :, h * 256:(h + 2) * 256], w=[wuqk], sem="w%d_0" % sl)
            P.dma("pool", wukv, wukv_d[:, :, h * 256:(h + 2) * 256], w=[wukvk], sem="w%d_1" % sl)
        for t in range(NT):
            b = psum(C, 4)
            for k in range(6):
                P.pe(lambda e, b=b, k=k, t=t, hh=hh, wuq=wuq: e.matmul(
                    C.psb[b][:], lhsT=wuq[:, k, hh * 256:hh * 256 + 128], rhs=C.H[:, t, k, :],
                    start=(k == 0), stop=(k == 5)), r=[wuqk, ("H", t, k)], w=[("ps", b)])
            P.act(lambda e, b=b, t=t: e.activation(out=qn[:, tsl(t)], in_=C.psb[b][:], func=AF.Copy),
                  r=[("ps", b)], w=[("B", "qn", t)])
            b = psum(C, 4)
            for k in range(6):
                P.pe(lambda e, b=b, k=k, t=t, hh=hh, wuq=wuq: e.matmul(
                    C.psb[b][:], lhsT=wuq[:, k, hh * 256 + 128:hh * 256 + 256], rhs=C.H[:, t, k, :],
                    start=(k == 0), stop=(k == 5)), r=[wuqk, ("H", t, k)], w=[("ps", b)])
            emit_rope(C, b, qpe[0:64, tsl(t)], ("B", "qpe", t), t)
            b = psum(C, 4)
            for k in range(2):
                P.pe(lambda e, b=b, k=k, t=t, hh=hh, wukv=wukv: e.matmul(
                    C.psb[b][:], lhsT=wukv[:, k, hh * 256:hh * 256 + 128], rhs=C.H[:, t, 6 + k, :],
                    start=(k == 0), stop=(k == 1)), r=[wukvk, ("H", t, 6 + k)], w=[("ps", b)])
            P.act(lambda e, b=b, t=t: e.activation(out=kn[:, tsl(t)], in_=C.psb[b][:], func=AF.Copy),
                  r=[("ps", b)], w=[("B", "kn", t)])
            b = psum(C, 4)
            for c4 in range(4):
                for k in range(2):
                    P.pe(lambda e, b=b, k=k, t=t, hh=hh, c4=c4, wukv=wukv: e.matmul(
                        C.psb[b][:, c4 * 128:(c4 + 1) * 128], lhsT=C.H[:, t, 6 + k, c4 * 128:(c4 + 1) * 128],
                        rhs=wukv[:, k, hh * 256 + 128:hh * 256 + 256], start=(k == 0), stop=(k == 1)),
                        r=[wukvk, ("H", t, 6 + k)], w=[("ps", b)])
            P.act(lambda e, b=b, t=t: e.activation(
                out=vv[:, 4 * t:4 * t + 4, :], in_=C.psb[b][:].rearrange("p (c n) -> p c n", c=4), func=AF.Copy),
                r=[("ps", b)], w=[("B", "v", t)])
        def S(kt, qt):
            b = kt % 3
            P.pe(lambda e: e.matmul(C.psb[b][:], lhsT=kn[:, kt * 128:(kt + 1) * 128], rhs=qn[:, tsl(qt)],
                                    start=True, stop=False),
                 r=[("B", "kn", kt // 4), ("B", "qn", qt)], w=[("ps", b)])
            P.pe(lambda e: e.matmul(C.psb[b][:], lhsT=kpe[:, kt * 128:(kt + 1) * 128], rhs=qpe[:, tsl(qt)],
                                    start=False, stop=True),
                 r=[("B", "kpe", kt // 4), ("B", "qpe", qt), ("B", "kpe_pad"), ("B", "qpe_pad")], w=[("ps", b)])

        accb = C.sq[:, 0, :]
        S(0, 0)
        S(1, 0)
        pending_fin = None
        for qt in range(NT):
            po = 4 + qt % 2
            pd = 6 + qt % 2
            for kt in range(16):
                b = kt % 3
                pi = kt % 4
                P.act(lambda e, b=b, pi=pi: e.activation(out=PT[:, pi, :], in_=C.psb[b][:], func=AF.Exp, scale=sm_scale),
                      r=[("ps", b)], w=[("B", "pt", pi)])
                if kt + 2 < 16:
                    S(kt + 2, qt)
                P.pe(lambda e, kt=kt, pi=pi, po=po: e.matmul(C.psb[po][:], lhsT=vv[:, kt, :], rhs=PT[:, pi, :],
                                                            start=(kt == 0), stop=(kt == 15)),
                     r=[("B", "v", kt // 4), ("B", "pt", pi)], w=[("ps", po)])
                if kt == 2 and pending_fin is not None:
                    pending_fin()
                    pending_fin = None
                if kt in (7, 15):
                    P.pe(lambda e, kt=kt, pi=pi, pd=pd: e.matmul(C.psb[pd][:], lhsT=C.ones_bf[:], rhs=PT[:, pi, :],
                                                                start=(kt == 7), stop=False),
                         r=["ones_bf", ("B", "pt", pi)], w=[("ps", pd)])
                elif kt == 0:
                    P.dve(lambda e, pi=pi: e.tensor_copy(out=acc, in_=PT[:, pi, :]), r=[("B", "pt", pi)], w=[("tmpf", 0)])
                elif kt == 14:
                    P.dve(lambda e, pi=pi: e.tensor_tensor(out=accb, in0=acc, in1=PT[:, pi, :], op=ALU.add),
                          r=[("B", "pt", pi), ("tmpf", 0)], w=[("sq", 0)])
                else:
                    P.dve(lambda e, pi=pi: e.tensor_tensor(out=acc, in0=acc, in1=PT[:, pi, :], op=ALU.add),
                          r=[("B", "pt", pi), ("tmpf", 0)], w=[("tmpf", 0)])
            if qt + 1 < NT:
                S(0, qt + 1)
                S(1, qt + 1)

            def finalize(po=po, pd=pd, qt=qt, h=h):
                P.pe(lambda e: e.matmul(C.psb[pd][:], lhsT=C.ones_bf[:], rhs=accb, start=False, stop=True),
                     r=["ones_bf", ("sq", 0)], w=[("ps", pd)])
                P.act(lambda e: e.activation(out=C.rstd[:], in_=C.psb[pd][:], func=AF.Ln), r=[("ps", pd)], w=["rstd"])
                P.act(lambda e: e.activation(out=C.tmpf[:, 2, :], in_=C.rstd[:], func=AF.Exp, scale=-1.0),
                      r=["rstd"], w=[("tmpf", 2)])
                P.dve(lambda e: e.tensor_tensor(out=attn[:, h, tsl(qt)], in0=C.psb[po][:], in1=C.tmpf[:, 2, :],
                                                op=ALU.mult),
                      r=[("ps", po), ("tmpf", 2)], w=[("A", "attn", h, qt)])
            finalize()
    wos = [load_w(C, wout_d[:, 4 * g:4 * g + 4, :], (4, D)) for g in range(2)]
    for t in range(NT):
        if hook is not None and t >= 1:
            hook(t - 1)
        for i in range(KC):
            b = psum(C, 4)
            for h in range(8):
                wo, wok = wos[h // 4]
                P.pe(lambda e, b=b, i=i, t=t, h=h, wo=wo: e.matmul(
                    C.psb[b][:], lhsT=wo[:, h % 4, i * 128:(i + 1) * 128], rhs=attn[:, h, tsl(t)],
                    start=(h == 0), stop=(h == 7)), r=[wok, ("A", "attn", h, t)], w=[("ps", b)])
            P.dve(lambda e, b=b, i=i, t=t: e.tensor_tensor(out=C.X[:, i, tsl(t)], in0=C.psb[b][:],
                                                           in1=C.X[:, i, tsl(t)], op=ALU.add),
                  r=[("ps", b), ("X", i, t)], w=[("X", i, t)])
    if hook is not None:
        hook(NT - 1)


def emit_gla(C, li, prenormed=False, hook=None):
    P = C.P
    j = li // 2
    if not prenormed:
        emit_norm_to_H(C, VC_MIX + li * 8)
    P.fence("A")
    P.fence("B")
    if getattr(C, "gla_ready", None) != li:
        emit_gla_setup(C, li)
    P.fence("R")
    A, B = C.A, C.B
    qf, kf, qb, kb = A[:, 0:2048], A[:, 2048:4096], A[:, 4096:6144], A[:, 6144:8192]
    vv = A[:, 8192:12288].rearrange("p (c n) -> p c n", c=16)
    keb = A[:, 12288:14336].rearrange("p (c n) -> p c n", c=16)
    ms = A[:, 14336:14848].rearrange("p (i n) -> p i n", i=2)
    kef = A[:, 14848:15104].rearrange("p (i n) -> p i n", i=2)
    on = A[:, 15104:16128].rearrange("p (v n) -> p v n", v=2)
    dec = A[:, 16128:16192].bitcast(F32).rearrange("p (c d) -> p c d", c=16)
    Sf = B[:, 0:4096].rearrange("p (c n) -> p c n", c=16)
    Sb = B[:, 4096:8192].rearrange("p (c n) -> p c n", c=16)
    stf = B[:, 8192:9216].bitcast(F32).rearrange("p (d n) -> p d n", d=2)
    la = B[:, 9216:9728].rearrange("p (i n) -> p i n", i=2)
    otmp = B[:, 10240:11264].bitcast(F32)
    ktok = otmp.rearrange("p (i n) -> p i n", i=4)
    e1 = B[:, 11264:11776].bitcast(F32)
    ee = B[:, 11776:12288].bitcast(F32)
    gflat = C.R[:, :].bitcast(BF16)
    gaug = gflat.rearrange("p (d n) -> p d n", d=2)
    tri = lambda i: C.tri_bf[:, i * 128:(i + 1) * 128]
    dk_scale = 128.0 ** -0.5
    NS = -1.0 / 16.0

    win = C.gla_win[j].rearrange("(k p) n -> p k n", p=128)
    wo_d = C.gla_wout[j].rearrange("(k p) n -> p k n", p=128)
    wg, wgk = load_w(C, win[:, :, 4 * GLA_HC:4 * GLA_HC + 32], (KC, 32))
    for t in range(NT):
        for d in range(2):
            b = 6 + d
            for k in range(KC):
                P.pe(lambda e, b=b, k=k, d=d, t=t: e.matmul(
                    C.psb[b][0:16, :], lhsT=wg[:, k, d * 16:(d + 1) * 16], rhs=C.H[:, t, k, :],
                    start=(k == 0), stop=(k == KC - 1)), r=[wgk, ("H", t, k)], w=[("ps", b)])
            P.act(lambda e, b=b, d=d, t=t: e.activation(out=gaug[0:16, d, tsl(t)], in_=C.psb[b][0:16, :], func=AF.Copy),
                  r=[("ps", b)], w=[("R", "g", d, t)])

    for h in range(4):
        w1, w1k = load_w(C, win[:, :, h * GLA_HC:h * GLA_HC + 512], (KC, 512))
        w2, w2k = load_w(C, win[:, :, h * GLA_HC + 512:h * GLA_HC + 768], (KC, 256))
        w3, w3k = load_w(C, wo_d[:, 2 * h:2 * h + 2, :], (2, D))
        P.dve(lambda e: e.memset(stf[:, 0, :], 0.0), w=[("B", "st", 0)])

        def proj(t, w1=w1, w1k=w1k):
            for b, c0 in ((0, 0), (1, 128)):
                for k in range(KC):
                    P.pe(lambda e, b=b, c0=c0, k=k: e.matmul(
                        C.psb[b][:], lhsT=w1[:, k, c0:c0 + 128], rhs=C.H[:, t, k, :],
                        start=(k == 0), stop=(k == KC - 1)), r=[w1k, ("H", t, k)], w=[("ps", b)])

        def stA(c, w1=w1, w1k=w1k, h=h):
            t, c4 = c // 4, c % 4
            lb = c % 2
            bt = 2 + lb
            cs = slice(c4 * 128, (c4 + 1) * 128)
            for k in range(KC):
                P.pe(lambda e, k=k: e.matmul(
                    C.psb[bt][:, 0:384], lhsT=C.H[:, t, k, cs], rhs=w1[:, k, 128:512],
                    start=(k == 0), stop=(k == KC - 1)), r=[w1k, ("H", t, k)], w=[("ps", bt)])
            for d in range(2):
                P.pe(lambda e, d=d: e.matmul(
                    C.psb[6][:, d * 128:(d + 1) * 128], lhsT=gaug[:, d, c * 128:(c + 1) * 128],
                    rhs=C.g2[:, d * 512 + h * 128:d * 512 + (h + 1) * 128], start=True, stop=True),
                    r=["g2", ("R", "g", d, c // 4)], w=[("ps", 6)])
            P.act(lambda e: e.activation(out=e1, in_=C.psb[6][:, 0:256], func=AF.Exp, scale=-1.0),
                  r=[("ps", 6)], w=[("B", "e1")])
            P.act(lambda e: e.activation(out=la[:, lb, :], in_=e1, func=AF.Ln, bias=1.0),
                  r=[("B", "e1")], w=[("B", "la", lb)])
            P.act(lambda e: e.activation(out=vv[:, c, :], in_=C.psb[bt][:, 128:384], func=AF.Copy),
                  r=[("ps", bt)], w=[("A", "v", c)])
            P.dve(lambda e: e.tensor_copy(out=ktok[:, c % 4, :], in_=C.psb[bt][:, 0:128]),
                  r=[("ps", bt)], w=[("B", "otmp", c % 4)])

        def stB(c):
            c4 = c % 4
            lb = c % 2
            bt = 2 + lb
            cs = slice(c4 * 128, (c4 + 1) * 128)
            P.pe(lambda e: e.matmul(C.psb[4][:, cs], lhsT=la[:, lb, 0:128], rhs=tri(0), start=True, stop=True),
                 r=[("B", "la", lb), "tri_bf"], w=[("ps", 4)])
            P.pe(lambda e: e.matmul(C.psb[5][:, cs], lhsT=la[:, lb, 128:256], rhs=tri(2), start=True, stop=True),
                 r=[("B", "la", lb), "tri_bf"], w=[("ps", 5)])
            P.pe(lambda e: e.matmul(C.psb[6][:, 256:384], lhsT=tri(1), rhs=la[:, lb, 0:128], start=True, stop=True),
                 r=[("B", "la", lb), "tri_bf"], w=[("ps", 6)])
            P.pe(lambda e: e.matmul(C.psb[6][:, 384:512], lhsT=tri(3), rhs=la[:, lb, 128:256], start=True, stop=True),
                 r=[("B", "la", lb), "tri_bf"], w=[("ps", 6)])
            P.act(lambda e: e.activation(out=ee, in_=C.psb[6][:, 256:512], func=AF.Exp, scale=NS),
                  r=[("ps", 6)], w=[("B", "ee")])
            P.dve(lambda e: e.tensor_tensor(out=kef[:, lb, :], in0=ktok[:, c % 4, :], in1=ee[:, 0:128], op=ALU.mult),
                  r=[("B", "otmp", c % 4), ("B", "ee")], w=[("A", "kef", lb)])
            P.dve(lambda e: e.tensor_tensor(out=keb[:, c, :], in0=ktok[:, c % 4, :], in1=ee[:, 128:256], op=ALU.mult),
                  r=[("B", "otmp", c % 4), ("B", "ee")], w=[("A", "keb", c)])
            P.act(lambda e: e.activation(out=dec[:, c, 0:1], in_=C.psb[4][:, c4 * 128 + 127:c4 * 128 + 128],
                                         func=AF.Exp, scale=NS), r=[("ps", 4)], w=[("A", "dec", c, 0)])
            P.act(lambda e: e.activation(out=dec[:, c, 1:2], in_=C.psb[5][:, c4 * 128:c4 * 128 + 1],
                                         func=AF.Exp, scale=NS), r=[("ps", 5)], w=[("A", "dec", c, 1)])

        cur = [0]

        def stC(c):
            lb = c % 2
            a, b2 = cur[0], 1 - cur[0]
            cur[0] = b2
            P.dve(lambda e: e.tensor_copy(out=Sf[:, c, :], in_=stf[:, a, :]),
                  r=[("B", "st", a)], w=[("B", "Sf", c)])
            P.pe(lambda e: e.matmul(C.psb[7][:, 0:256], lhsT=kef[:, lb, :], rhs=vv[:, c, :], start=True, stop=True),
                 r=[("A", "kef", lb), ("A", "v", c)], w=[("ps", 7)])
            P.dve(lambda e: e.scalar_tensor_tensor(out=stf[:, b2, :], in0=stf[:, a, :], scalar=dec[:, c, 0:1],
                                                   in1=C.psb[7][:, 0:256], op0=ALU.mult, op1=ALU.add),
                  r=[("B", "st", a), ("A", "dec", c, 0), ("ps", 7)], w=[("B", "st", b2)])

        def tile_end(t):
            tmps = [(C.tmpf[:, 0, :], ("tmpf", 0)), (C.tmpf[:, 1, :], ("tmpf", 1)), (C.tmpf[:, 2, :], ("tmpf", 2)),
                    (C.rstd[:], ("rstd",))]
            jobs = [(4, NS, 0, qf, "qf"), (4, -NS, 1, kf, "kf"), (5, NS, 0, qb, "qb"), (5, -NS, 1, kb, "kb")]
            for (bb, sc, src, dst, nm), (tb, tk) in zip(jobs, tmps):
                P.act(lambda e, bb=bb, sc=sc, tb=tb: e.activation(out=tb, in_=C.psb[bb][:], func=AF.Exp, scale=sc),
                      r=[("ps", bb)], w=[tk])
            for (bb, sc, src, dst, nm), (tb, tk) in zip(jobs, tmps):
                if src == 0:
                    P.dve(lambda e, dst=dst, tb=tb: e.scalar_tensor_tensor(
                        out=dst[:, tsl(t)], in0=C.psb[0][:], scalar=dk_scale, in1=tb, op0=ALU.mult, op1=ALU.mult),
                        r=[("ps", 0), tk], w=[("A", nm, t)])
                else:
                    P.dve(lambda e, dst=dst, tb=tb: e.tensor_tensor(out=dst[:, tsl(t)], in0=C.psb[1][:], in1=tb,
                                                                    op=ALU.mult),
                          r=[("ps", 1), tk], w=[("A", nm, t)])

        proj(0)
        for step in range(16 + 3):
            if 0 <= step - 2 < 16:
                stB(step - 2)
                if (step - 2) % 4 == 3:
                    tile_end((step - 2) // 4)
            if 0 <= step - 3 < 16:
                stC(step - 3)
            if step < 16:
                stA(step)
            if step >= 7 and (step - 7) % 4 == 0 and (step - 7) // 4 + 1 < NT:
                proj((step - 7) // 4 + 1)

        order = list(reversed(range(16)))

        def kvb(i):
            c = order[i]
            bk = 6 + i % 2
            P.pe(lambda e: e.matmul(C.psb[bk][:, 0:256], lhsT=keb[:, c, :], rhs=vv[:, c, :], start=True, stop=True),
                 r=[("A", "keb", c), ("A", "v", c)], w=[("ps", bk)])
        kvb(0)
        P.dve(lambda e: e.memset(stf[:, 0, :], 0.0), w=[("B", "st", 0)])
        cur[0] = 0
        for i, c in enumerate(order):
            bk = 6 + i % 2
            a, b2 = cur[0], 1 - cur[0]
            cur[0] = b2
            P.act(lambda e, c=c, a=a: e.activation(out=Sb[:, c, :], in_=stf[:, a, :], func=AF.Copy),
                  r=[("B", "st", a)], w=[("B", "Sb", c)])
            if i + 1 < 16:
                kvb(i + 1)
            P.dve(lambda e, c=c, bk=bk, a=a, b2=b2: e.scalar_tensor_tensor(
                out=stf[:, b2, :], in0=stf[:, a, :], scalar=dec[:, c, 1:2], in1=C.psb[bk][:, 0:256],
                op0=ALU.mult, op1=ALU.add),
                r=[("B", "st", a), ("A", "dec", c, 1), ("ps", bk)], w=[("B", "st", b2)])

        def chunks(t, extras=()):
            extras = list(extras)
            pob = (1, 2) if t % 2 == 0 else (5, 6)

            def scores(c):
                sb_ = 0 if c % 2 == 0 else 7
                cs = slice(c * 128, (c + 1) * 128)
                P.pe(lambda e: e.matmul(C.psb[sb_][:, 0:128], lhsT=kf[:, cs], rhs=qf[:, cs], start=True, stop=True),
                     r=[("A", "kf", t), ("A", "qf", t)], w=[("ps", sb_)])
                P.pe(lambda e: e.matmul(C.psb[sb_][:, 128:256], lhsT=kb[:, cs], rhs=qb[:, cs], start=True, stop=True),
                     r=[("A", "kb", t), ("A", "qb", t)], w=[("ps", sb_)])
            scores(4 * t)
            for c4 in range(4):
                c = 4 * t + c4
                mb = c % 2
                sb_ = 0 if c % 2 == 0 else 7
                cs = slice(c * 128, (c + 1) * 128)
                if c4 + 1 < 4:
                    scores(c + 1)
                P.dve(lambda e, mb=mb, sb_=sb_: e.tensor_tensor(out=ms[:, mb, :], in0=C.psb[sb_][:, 0:256],
                                                                in1=C.tri_bf[:, 0:256], op=ALU.mult),
                      r=[("ps", sb_), "tri_bf"], w=[("A", "ms", mb)])
                for vc in range(2):
                    ob = pob[vc]
                    osl = slice(c4 * 128, (c4 + 1) * 128)
                    vs = slice(vc * 128, (vc + 1) * 128)
                    P.pe(lambda e, ob=ob, osl=osl, vs=vs, c=c, mb=mb: e.matmul(
                        C.psb[ob][:, osl], lhsT=vv[:, c, vs], rhs=ms[:, mb, 0:128], start=True, stop=False),
                        r=[("A", "v", c), ("A", "ms", mb)], w=[("ps", ob)])
                    P.pe(lambda e, ob=ob, osl=osl, vs=vs, c=c, mb=mb: e.matmul(
                        C.psb[ob][:, osl], lhsT=vv[:, c, vs], rhs=ms[:, mb, 128:256], start=False, stop=False),
                        r=[("A", "v", c), ("A", "ms", mb)], w=[("ps", ob)])
                    P.pe(lambda e, ob=ob, osl=osl, vs=vs, c=c, cs=cs: e.matmul(
                        C.psb[ob][:, osl], lhsT=Sf[:, c, vs], rhs=qf[:, cs], start=False, stop=False),
                        r=[("B", "Sf", c), ("A", "qf", t)], w=[("ps", ob)])
                    P.pe(lambda e, ob=ob, osl=osl, vs=vs, c=c, cs=cs: e.matmul(
                        C.psb[ob][:, osl], lhsT=Sb[:, c, vs], rhs=qb[:, cs], start=False, stop=True),
                        r=[("B", "Sb", c), ("A", "qb", t)], w=[("ps", ob)])
                for _ in range(2):
                    if extras:
                        extras.pop(0)()
            while extras:
                extras.pop(0)()

        def R_pieces(t, w2=w2, w2k=w2k):
            out = []
            for vc in range(2):
                b = 3 + vc
                for k0 in range(0, KC, 2):
                    def mm(k0=k0, vc=vc, b=b):
                        for k in (k0, k0 + 1):
                            P.pe(lambda e, k=k: e.matmul(
                                C.psb[b][:], lhsT=w2[:, k, vc * 128:(vc + 1) * 128], rhs=C.H[:, t, k, :],
                                start=(k == 0), stop=(k == KC - 1)), r=[w2k, ("H", t, k)], w=[("ps", b)])
                    out.append(mm)
                out.append(lambda vc=vc, b=b: P.act(
                    lambda e: e.activation(out=C.tmpf[:, vc, :], in_=C.psb[b][:], func=AF.Silu),
                    r=[("ps", b)], w=[("tmpf", vc)]))
            return out

        otmp2 = B[:, 10240:12288].bitcast(F32).rearrange("p (v n) -> p v n", v=2)
        o2keys = ([("B", "otmp")], [("B", "e1"), ("B", "ee")])

        def N_a(t):
            pob = (1, 2) if t % 2 == 0 else (5, 6)
            for vc in range(2):
                P.act(lambda e, vc=vc: e.activation(out=C.sq[:, vc, :], in_=C.psb[pob[vc]][:], func=AF.Square),
                      r=[("ps", pob[vc])], w=[("sq", vc)])
                P.pe(lambda e, vc=vc: e.matmul(C.psb[3][:], lhsT=C.ones_bf[:], rhs=C.sq[:, vc, :],
                                               start=(vc == 0), stop=(vc == 1)),
                     r=[("sq", vc), "ones_bf"], w=[("ps", 3)])
            for vc in range(2):
                P.dve(lambda e, vc=vc: e.tensor_tensor(out=otmp2[:, vc, :], in0=C.psb[pob[vc]][:], in1=C.tmpf[:, vc, :],
                                                       op=ALU.mult),
                      r=[("ps", pob[vc]), ("tmpf", vc)], w=o2keys[vc])

        def N_b(t):
            P.act(lambda e: e.activation(out=C.tmpf[:, 2, :], in_=C.psb[3][:], func=AF.Ln, bias=EPS, scale=1.0 / 256),
                  r=[("ps", 3)], w=[("tmpf", 2)])
            P.act(lambda e: e.activation(out=C.rstd[:], in_=C.tmpf[:, 2, :], func=AF.Exp, scale=-0.5),
                  r=[("tmpf", 2)], w=["rstd"])
            for vc in range(2):
                gc = VC_HN + j * 2 + vc
                P.dve(lambda e, vc=vc, gc=gc: e.scalar_tensor_tensor(
                    out=on[:, vc, :], in0=otmp2[:, vc, :], scalar=C.vecs[:, gc:gc + 1], in1=C.rstd[:],
                    op0=ALU.mult, op1=ALU.mult), r=o2keys[vc] + ["vecs", "rstd"], w=[("A", "on", vc)])

        def W_pieces(t, w3=w3, w3k=w3k):
            def piece(i):
                b = 0 if i % 2 == 0 else 7
                for vc in range(2):
                    P.pe(lambda e, vc=vc: e.matmul(
                        C.psb[b][:], lhsT=w3[:, vc, i * 128:(i + 1) * 128], rhs=on[:, vc, :],
                        start=(vc == 0), stop=(vc == 1)), r=[w3k, ("A", "on", vc)], w=[("ps", b)])
                P.dve(lambda e: e.tensor_tensor(out=C.X[:, i, tsl(t)], in0=C.psb[b][:], in1=C.X[:, i, tsl(t)], op=ALU.add),
                      r=[("ps", b), ("X", i, t)], w=[("X", i, t)])
            return [lambda i=i: piece(i) for i in range(KC)]

        for pc in R_pieces(0):
            pc()
        chunks(0)
        chunks(1)
        for t in range(NT):
            N_a(t)
            if t + 2 < NT:
                chunks(t + 2)
            N_b(t)
            wp = W_pieces(t)
            rp = R_pieces(t + 1) if t + 1 < NT else []
            for _ in range(min(5, len(rp))):
                rp.pop(0)()
            while wp or rp:
                if wp:
                    wp.pop(0)()
                if rp:
                    rp.pop(0)()
            if h == 3 and hook is not None:
                hook(t, bank=3)


FULL_SPEC = []
for _li in range(DEPTH):
    FULL_SPEC.append(("ffn", _li, 0))
    FULL_SPEC.append(("gla", _li) if _li % 2 == 0 else ("mla", _li))
    FULL_SPEC.append(("ffn", _li, 1))
FULL_SPEC.append(("final",))


def kernel(**inputs):
    per_core = _host_prepare(inputs)
    nc = build_program(FULL_SPEC)
    res = run_bass_kernel_spmd(nc, per_core, core_ids=list(range(len(per_core))))
    out = np.stack([np.ascontiguousarray(r["outT"].T) for r in res.results], axis=0)
    return out.astype(np.float32)
```
